# Optimizing a Trainium2 kernel written in Bass

```python
import math
import jax, jax.numpy as jnp
from jax import lax
import numpy as np

D_MODEL = 1024
BATCH = 8
SEQ = 2048
DEPTH = 1

SSM_WIDTH = D_MODEL // 2
SSM_GROUP = 16
SSM_GROUPS = SSM_WIDTH // SSM_GROUP
SSM_STATE = 64
N_HEADS = 8
HEAD_DIM = 64
ATTN_WIDTH = N_HEADS * HEAD_DIM
KV_DIM = HEAD_DIM
IDX_HEADS = 4
IDX_DIM = 64
INDEX_TOPK = 256
Q_BLOCK = 128
N_BRANCH = 2
D_FF = -(-8 * D_MODEL // (3 * 256)) * 256
RMS_EPS = 1e-6
DT_MIN = 1e-3
DT_MAX = 1e-1

IN_SIZES = (SSM_WIDTH, ATTN_WIDTH, KV_DIM, KV_DIM, IDX_HEADS * IDX_DIM, IDX_DIM, IDX_HEADS, N_BRANCH * D_MODEL)
IN_COLS = sum(IN_SIZES)
IN_SPLITS = tuple(int(s) for s in np.cumsum(IN_SIZES)[:-1])

kernel_name = "hybrid_s5_dsa_gated_block"


def _rms(x, g, eps=RMS_EPS):
    xf = x.astype(jnp.float32)
    y = xf * lax.rsqrt(jnp.mean(xf * xf, axis=-1, keepdims=True) + eps)
    return (y * g.astype(jnp.float32)).astype(x.dtype)


def _complex_combine(e1, e2):
    a1r, a1i, b1r, b1i = e1
    a2r, a2i, b2r, b2i = e2
    return (a2r * a1r - a2i * a1i,
            a2r * a1i + a2i * a1r,
            a2r * b1r - a2i * b1i + b2r,
            a2r * b1i + a2i * b1r + b2i)


def _s5_branch(u, A_re, A_im, log_dt, B_re, B_im, C_re, C_im, D_skip, w_glu, b_glu):
    bsz, L, _ = u.shape
    uf = u.astype(jnp.float32).reshape(bsz, L, SSM_GROUPS, SSM_GROUP)
    ar = A_re.astype(jnp.float32)
    ai = A_im.astype(jnp.float32)
    dt = jnp.exp(log_dt.astype(jnp.float32))[:, None]
    mag = jnp.exp(ar * dt)
    abar_r = mag * jnp.cos(ai * dt)
    abar_i = mag * jnp.sin(ai * dt)
    den = ar * ar + ai * ai
    nr = abar_r - 1.0
    coef_r = (nr * ar + abar_i * ai) / den
    coef_i = (abar_i * ar - nr * ai) / den
    bu_r = jnp.einsum('gnp,blgp->blgn', B_re.astype(jnp.float32), uf)
    bu_i = jnp.einsum('gnp,blgp->blgn', B_im.astype(jnp.float32), uf)
    b_r = coef_r * bu_r - coef_i * bu_i
    b_i = coef_r * bu_i + coef_i * bu_r
    a_r = jnp.broadcast_to(abar_r, (1, L, SSM_GROUPS, SSM_STATE))
    a_i = jnp.broadcast_to(abar_i, (1, L, SSM_GROUPS, SSM_STATE))
    _, _, s_r, s_i = lax.associative_scan(_complex_combine, (a_r, a_i, b_r, b_i), axis=1)
    y = (jnp.einsum('gpn,blgn->blgp', C_re.astype(jnp.float32), s_r)
         - jnp.einsum('gpn,blgn->blgp', C_im.astype(jnp.float32), s_i)
         + D_skip.astype(jnp.float32).reshape(SSM_GROUPS, SSM_GROUP) * uf)
    y = jax.nn.gelu(y.reshape(bsz, L, SSM_WIDTH))
    y = y * jax.nn.sigmoid(y @ w_glu.astype(jnp.float32) + b_glu.astype(jnp.float32))
    return y.astype(u.dtype)


def _dsa_branch(q, k, v, qi, ki, wi):
    bsz, L = q.shape[0], q.shape[1]
    k_sel = min(INDEX_TOPK, L // 4)
    nb = L // Q_BLOCK
    q_blk = q.reshape(bsz, nb, Q_BLOCK, N_HEADS, HEAD_DIM).transpose(1, 0, 2, 3, 4)
    qi_blk = qi.reshape(bsz, nb, Q_BLOCK, IDX_HEADS, IDX_DIM).transpose(1, 0, 2, 3, 4)
    wi_blk = wi.reshape(bsz, nb, Q_BLOCK, IDX_HEADS).transpose(1, 0, 2, 3)
    starts = jnp.arange(nb, dtype=jnp.int32) * Q_BLOCK
    kif = ki.astype(jnp.float32)
    key_pos = jnp.arange(L, dtype=jnp.int32)
    w_scale = (IDX_HEADS ** -0.5) * (IDX_DIM ** -0.5)
    a_scale = HEAD_DIM ** -0.5
    gather = jax.vmap(lambda kk, ii: kk[ii])

    def block(args):
        qb, qib, wib, start = args
        pos_q = start + jnp.arange(Q_BLOCK, dtype=jnp.int32)
        visible = key_pos[None, :] <= pos_q[:, None]
        rel = jax.nn.relu(jnp.einsum('bqhd,bsd->bqhs', qib.astype(jnp.float32), kif))
        iscore = jnp.einsum('bqhs,bqh->bqs', rel, wib.astype(jnp.float32) * w_scale)
        iscore = jnp.where(visible[None], iscore, -jnp.inf)
        _, idx = lax.top_k(iscore, k_sel)
        valid = idx <= pos_q[None, :, None]
        kg = gather(k, idx).astype(jnp.float32)
        vg = gather(v, idx).astype(jnp.float32)
        logits = jnp.einsum('bqhd,bqkd->bhqk', qb.astype(jnp.float32), kg) * a_scale
        logits = jnp.where(valid[:, None], logits, -jnp.inf)
        p = jax.nn.softmax(logits, axis=-1)
        o = jnp.einsum('bhqk,bqkd->bqhd', p, vg)
        return o.astype(q.dtype)

    out = lax.map(block, (q_blk, qi_blk, wi_blk, starts))
    return out.transpose(1, 0, 2, 3, 4).reshape(bsz, L, ATTN_WIDTH)


def setup_inputs(seed: int = 0) -> dict:
    key = jax.random.key(seed)
    ks = jax.random.split(key, 24)
    f32 = jnp.float32
    nrm = lambda k, shape, s: jax.random.normal(k, shape, f32) * s
    gain = lambda k, shape: 1.0 + 0.02 * jax.random.normal(k, shape, f32)
    Ld = DEPTH
    G, N, P = SSM_GROUPS, SSM_STATE, SSM_GROUP
    a_im = jnp.broadcast_to(math.pi * jnp.arange(N, dtype=f32), (Ld, G, N)) + 0.01 * jax.random.normal(ks[4], (Ld, G, N), f32)
    log_dt = jax.random.uniform(ks[5], (Ld, G), f32, math.log(DT_MIN), math.log(DT_MAX))
    return {
        "x": jax.random.normal(ks[0], (BATCH, SEQ, D_MODEL), f32),
        "norm1_g": gain(ks[1], (Ld, D_MODEL)),
        "w_in": nrm(ks[2], (Ld, D_MODEL, IN_COLS), D_MODEL ** -0.5),
        "A_re": -0.5 + 0.01 * jax.random.normal(ks[3], (Ld, G, N), f32),
        "A_im": a_im,
        "log_dt": log_dt,
        "B_re": nrm(ks[6], (Ld, G, N, P), (2 * P) ** -0.5),
        "B_im": nrm(ks[7], (Ld, G, N, P), (2 * P) ** -0.5),
        "C_re": nrm(ks[8], (Ld, G, P, N), (2 * N) ** -0.5),
        "C_im": nrm(ks[9], (Ld, G, P, N), (2 * N) ** -0.5),
        "D_skip": nrm(ks[10], (Ld, SSM_WIDTH), 1.0),
        "w_glu": nrm(ks[11], (Ld, SSM_WIDTH, SSM_WIDTH), SSM_WIDTH ** -0.5),
        "b_glu": nrm(ks[12], (Ld, SSM_WIDTH), 0.01),
        "q_norm_g": gain(ks[13], (Ld, HEAD_DIM)),
        "k_norm_g": gain(ks[14], (Ld, KV_DIM)),
        "idx_k_norm_g": gain(ks[15], (Ld, IDX_DIM)),
        "w_proj_ssm": nrm(ks[16], (Ld, SSM_WIDTH, D_MODEL), SSM_WIDTH ** -0.5),
        "w_proj_attn": nrm(ks[17], (Ld, ATTN_WIDTH, D_MODEL), ATTN_WIDTH ** -0.5),
        "w_out": nrm(ks[18], (Ld, D_MODEL, D_MODEL), D_MODEL ** -0.5),
        "norm2_g": gain(ks[19], (Ld, D_MODEL)),
        "w_ffn_gate": nrm(ks[20], (Ld, D_MODEL, D_FF), D_MODEL ** -0.5),
        "w_ffn_up": nrm(ks[21], (Ld, D_MODEL, D_FF), D_MODEL ** -0.5),
        "w_ffn_down": nrm(ks[22], (Ld, D_FF, D_MODEL), D_FF ** -0.5),
    }


def reference(x, norm1_g, w_in, A_re, A_im, log_dt, B_re, B_im, C_re, C_im, D_skip, w_glu, b_glu,
              q_norm_g, k_norm_g, idx_k_norm_g, w_proj_ssm, w_proj_attn, w_out,
              norm2_g, w_ffn_gate, w_ffn_up, w_ffn_down):
    bsz, L, _ = x.shape
    for l in range(DEPTH):
        h = _rms(x, norm1_g[l])
        proj = h @ w_in[l]
        u, q, k, v, qi, ki, wi, gates = jnp.split(proj, IN_SPLITS, axis=-1)
        y_ssm = _s5_branch(u, A_re[l], A_im[l], log_dt[l], B_re[l], B_im[l], C_re[l], C_im[l],
                           D_skip[l], w_glu[l], b_glu[l])
        q = _rms(q.reshape(bsz, L, N_HEADS, HEAD_DIM), q_norm_g[l])
        k = _rms(k, k_norm_g[l])
        ki = _rms(ki, idx_k_norm_g[l])
        qi = qi.reshape(bsz, L, IDX_HEADS, IDX_DIM)
        y_att = _dsa_branch(q, k, v, qi, ki, wi)
        g_ssm, g_att = jnp.split(gates, N_BRANCH, axis=-1)
        merged = (jax.nn.sigmoid(g_ssm) * (y_ssm @ w_proj_ssm[l])
                  + jax.nn.sigmoid(g_att) * (y_att @ w_proj_attn[l]))
        x = x + merged @ w_out[l]
        h2 = _rms(x, norm2_g[l])
        x = x + (jax.nn.silu(h2 @ w_ffn_gate[l]) * (h2 @ w_ffn_up[l])) @ w_ffn_down[l]
    return x
```

```python
import math
import numpy as np
import concourse.bass as bass
import concourse.mybir as mybir
from concourse.bass_utils import run_bass_kernel_spmd

F32 = mybir.dt.float32
BF16 = mybir.dt.bfloat16
I32 = mybir.dt.int32
AF = mybir.ActivationFunctionType
ALU = mybir.AluOpType
AX = mybir.AxisListType

ENGS = ("pe", "act", "dve", "pool", "sp")

L = 2048
D = 1024
NT = L // 128
DFF = 2816
NFC = DFF // 128
EPS = 1e-6
TM = 16
NM = L // TM
NEG = -30000.0


class Prog:
    def __init__(self, nc):
        self.nc = nc
        self.ops = []
        self.seq = {e: 0 for e in ENGS}
        self.lastw = {}
        self.readers = {}
        self.seen = {e: {} for e in ENGS}
        self.dma_lists = {}
        self.dma_kind = {}
        self.needed = set()
        self.last_real = {e: 0 for e in ENGS}

    def defer(self):
        self.deferq = []
        self.deferring = True

    def pause_defer(self):
        self.deferring = False

    def flush(self, n=None):
        q = getattr(self, 'deferq', None) or []
        n = len(q) if n is None else min(n, len(q))
        was = getattr(self, 'deferring', False)
        self.deferring = False
        for _ in range(n):
            args = q.pop(0)
            self._add(*args)
        self.deferring = was

    def _add(self, eng, fn, reads, writes, dma=None, dma_kind="slot"):
        if getattr(self, 'deferring', False):
            self.deferq.append((eng, fn, list(reads), list(writes), dma, dma_kind))
            return None
        self.seq[eng] += 1
        myseq = self.seq[eng]
        deps = {}

        def need(tok):
            if tok is None:
                return
            if tok[0] == 'e':
                _, pe, ps = tok
                if pe == 'pe' and eng == 'pe':
                    return
                k = ('e', pe)
                deps[k] = max(deps.get(k, 0), ps)
            else:
                _, sk, pos = tok
                if dma is not None and sk == dma and dma_kind == 'group':
                    return
                k = ('d', sk)
                deps[k] = max(deps.get(k, 0), pos)

        for k in reads:
            need(self.lastw.get(k))
        for k in writes:
            need(self.lastw.get(k))
            for r in self.readers.get(k, ()):
                need(r)
        waits = []
        seen = self.seen[eng]
        for k, v in deps.items():
            if seen.get(k, 0) >= v:
                continue
            seen[k] = v
            waits.append((k, v))
            if k[0] == 'e':
                self.needed.add((k[1], v))
        if dma is not None:
            lst = self.dma_lists.setdefault(dma, [])
            self.dma_kind[dma] = dma_kind
            lst.append(len(self.ops))
            tok = ('d', dma, len(lst))
        else:
            tok = ('e', eng, myseq)
            if fn is not None:
                self.last_real[eng] = myseq
        self.ops.append(dict(eng=eng, fn=fn, waits=waits, dma=dma, seq=myseq))
        for k in reads:
            self.readers.setdefault(k, []).append(tok)
        for k in writes:
            self.lastw[k] = tok
            self.readers[k] = []
        return tok

    def pe(self, fn, reads=(), writes=()):
        return self._add('pe', fn, reads, writes)

    def act(self, fn, reads=(), writes=()):
        return self._add('act', fn, reads, writes)

    def dve(self, fn, reads=(), writes=()):
        return self._add('dve', fn, reads, writes)

    def pool(self, fn, reads=(), writes=()):
        return self._add('pool', fn, reads, writes)

    def on(self, eng, fn, reads=(), writes=()):
        return self._add(eng, fn, reads, writes)

    def dma(self, eng, fn, reads=(), writes=(), sem="misc", kind="slot"):
        return self._add(eng, fn, reads, writes, dma=sem, dma_kind=kind)

    def barrier(self):
        snap_e = dict(self.last_real)
        snap_d = {k: len(v) for k, v in self.dma_lists.items()}
        for e in ENGS:
            waits = []
            seen = self.seen[e]
            for e2 in ENGS:
                if e2 == 'sp' or snap_e[e2] == 0:
                    continue
                if e2 == e and e == 'pe':
                    continue
                k = ('e', e2)
                if seen.get(k, 0) >= snap_e[e2]:
                    continue
                seen[k] = snap_e[e2]
                waits.append((k, snap_e[e2]))
                self.needed.add((e2, snap_e[e2]))
            for sk, pos in snap_d.items():
                k = ('d', sk)
                if seen.get(k, 0) >= pos:
                    continue
                seen[k] = pos
                waits.append((k, pos))
            self.ops.append(dict(eng=e, fn=None, waits=waits, dma=None, seq=-1))

    def final_wait(self, keys):
        self._add('sp', None, reads=keys, writes=())

    def emit(self):
        nc = self.nc
        semval = {}
        cnt = {e: 0 for e in ENGS}
        for op in self.ops:
            if op['dma'] is None and op['fn'] is not None and (op['eng'], op['seq']) in self.needed:
                cnt[op['eng']] += 1
                semval[(op['eng'], op['seq'])] = cnt[op['eng']]
        esem = {e: nc.alloc_semaphore("s_" + e) for e in ENGS}
        dsem = {}
        for i, k in enumerate(self.dma_lists):
            dsem[k] = nc.alloc_semaphore("d_%d" % i)

        def run(ename, eng):
            for op in self.ops:
                if op['eng'] != ename:
                    continue
                for (k, v) in op['waits']:
                    if k[0] == 'e':
                        eng.wait_ge(esem[k[1]], semval[(k[1], v)])
                    else:
                        sk = k[1]
                        if self.dma_kind[sk] == 'group':
                            eng.wait_ge(dsem[sk], 16 * len(self.dma_lists[sk]))
                        else:
                            eng.wait_ge(dsem[sk], 16 * v)
                if op['fn'] is None:
                    continue
                inst = op['fn'](eng)
                if op['dma'] is not None:
                    inst.then_inc(dsem[op['dma']], 16)
                elif (ename, op['seq']) in semval:
                    inst.then_inc(esem[ename], 1)

        with nc.Block() as block:
            @block.tensor
            def _(eng):
                run('pe', eng)

            @block.scalar
            def _(eng):
                run('act', eng)

            @block.vector
            def _(eng):
                run('dve', eng)

            @block.gpsimd
            def _(eng):
                run('pool', eng)

            @block.sync
            def _(eng):
                run('sp', eng)


class Region:
    def __init__(self, arena, start, end):
        self.arena, self.start, self.end, self.cur = arena, start, end, start

    def take(self, shape, dt):
        n = int(np.prod(shape[1:]))
        esz = 4 if dt in (F32, I32) else 2
        nb = (n * esz + 63) // 64 * 64
        off = self.cur
        self.cur += nb
        assert self.cur <= self.end, ("region overflow", shape, self.cur, self.end)
        v = self.arena[:, off // 2: off // 2 + (n * esz) // 2]
        if dt != BF16:
            v = v.bitcast(dt)
        if len(shape) == 3:
            v = v.rearrange("p (a b) -> p a b", a=shape[1])
        elif len(shape) == 4:
            v = v.rearrange("p (a b c) -> p a b c", a=shape[1], b=shape[2])
        if shape[0] < 128:
            v = v[0:shape[0]]
        return v


def build_nc(dbg=()):
    nc = bass.Bass("TRN2", target_bir_lowering=False)
    P = Prog(nc)

    def din(name, shape):
        return nc.dram_tensor(name, list(shape), F32, kind="ExternalInput").ap()

    x = din("x", [L, D])
    norm1_g = din("norm1_g", [1, D])
    w_in = din("w_in", [D, 3524])
    A_re = din("A_re", [32, 64])
    A_im = din("A_im", [32, 64])
    log_dt = din("log_dt", [32])
    B_re = din("B_re", [32, 64, 16])
    B_im = din("B_im", [32, 64, 16])
    C_re = din("C_re", [32, 16, 64])
    C_im = din("C_im", [32, 16, 64])
    D_skip = din("D_skip", [512])
    w_glu = din("w_glu", [512, 512])
    b_glu = din("b_glu", [512])
    q_norm_g = din("q_norm_g", [1, 64])
    k_norm_g = din("k_norm_g", [1, 64])
    idx_k_norm_g = din("idx_k_norm_g", [1, 64])
    w_proj_ssm = din("w_proj_ssm", [512, D])
    w_proj_attn = din("w_proj_attn", [512, D])
    w_out = din("w_out", [D, D])
    norm2_g = din("norm2_g", [1, D])
    w_ffn_gate = din("w_ffn_gate", [D, DFF])
    w_ffn_up = din("w_ffn_up", [D, DFF])
    w_ffn_down = din("w_ffn_down", [DFF, D])
    y = nc.dram_tensor("y", [L, D], F32, kind="ExternalOutput").ap()
    dbg_out = {}

    ARENA_BYTES = 210944
    arena = nc.alloc_sbuf_tensor("arena", [128, ARENA_BYTES // 2], BF16)
    psall = nc.alloc_psum_tensor("psall", [128, 8 * 512], F32)
    ps = [psall[:, i * 512:(i + 1) * 512] for i in range(8)]

    def psb(i):
        return ps[i].bitcast(BF16).rearrange("p (a b) -> p a b", a=8)

    RC_ = Region(arena, 0, 4096)
    XN_ = Region(arena, 4096, 36864)
    RCm = Region(arena, 36864, 69632)
    RA_ = Region(arena, 69632, 86016)
    RB_ = Region(arena, 86016, 102400)
    RD_ = Region(arena, 102400, 130048)
    SCR0, SCR1 = 130048, ARENA_BYTES

    ident_f = RC_.take([128, 128], F32)
    ident_bf = RC_.take([128, 128], BF16)
    cbias = RC_.take([128, 128], BF16)
    cneg = RC_.take([128, 128], F32)
    ident4 = RC_.take([128, 4, 128], BF16)
    iota_i = RC_.take([128, 128], I32)
    zeros_bf = RC_.take([128, 128], BF16)

    P.pool(lambda e: e.iota(iota_i, pattern=[[1, 128]], base=0, channel_multiplier=-1), writes=['iota'])
    P.dve(lambda e: e.tensor_single_scalar(out=ident_f, in_=iota_i, scalar=0.0, op=ALU.is_equal), reads=['iota'], writes=['ident_f'])
    P.dve(lambda e: e.tensor_copy(out=ident_bf, in_=ident_f), reads=['ident_f'], writes=['ident_bf'])
    P.dve(lambda e: e.tensor_scalar(out=cneg, in0=iota_i, scalar1=0.0, scalar2=-1.0e30, op0=ALU.is_gt, op1=ALU.mult), reads=['iota'], writes=['cneg'])
    P.dve(lambda e: e.tensor_scalar(out=cbias, in0=iota_i, scalar1=0.0, scalar2=NEG, op0=ALU.is_gt, op1=ALU.mult), reads=['iota'], writes=['cbias'])
    for j in range(4):
        P.dve(lambda e, j=j: e.tensor_copy(out=ident4[:, j, :], in_=ident_f), reads=['ident_f'], writes=['ident4'])
    P.dve(lambda e: e.memset(zeros_bf, 0.0), writes=['zeros'])

    xnT = XN_.take([128, 8, L], BF16)
    uT = RA_.take([128, 4, L], BF16)
    ysT = uT
    ygT = RB_.take([128, 4, L], BF16)
    yaT = ygT
    qT = RCm.take([128, 8, L], BF16)
    mergedT = qT
    qiT = RD_.take([64, 4, L], BF16)
    kT = RD_.take([64, L], BF16)
    kiT = RD_.take([64, L], BF16)
    vaug = RD_.take([128, NT, 65], BF16)
    wi_s = RD_.take([128, NT, 4], F32)

    def dump(name, ap, shape):
        if name in dbg:
            o = nc.dram_tensor("dbg_" + name, list(shape), ap.dtype, kind="ExternalOutput").ap()
            dbg_out[name] = o
            P.dma('sp', lambda e: e.dma_start(out=o, in_=ap), reads=list(P.lastw.keys()), writes=['dbg_' + name], sem=('dbg', name))

    w_in_v = w_in.rearrange("(kc p) n -> p kc n", p=128)
    W_SCALE = (4 ** -0.5) * (64 ** -0.5)

    def phase_ab(hook=None):
        S = Region(arena, SCR0, SCR1)
        Wu = S.take([128, 8, 512], BF16)
        g1b = S.take([128, D], F32)
        xbuf = [S.take([128, D], F32) for _ in range(3)]
        xnb = [S.take([128, D], BF16) for _ in range(3)]
        junk = S.take([128, D], BF16)
        ss1 = S.take([128, NT], F32)
        rt1 = S.take([128, NT], F32)
        rs1 = S.take([128, NT], F32)
        P.dma('pool', lambda e: e.dma_start(out=Wu, in_=w_in_v[:, :, 0:512]), writes=['Wu'], sem='wu')
        P.dma('sp', lambda e: e.dma_start(out=g1b, in_=norm1_g.to_broadcast([128, D])), writes=['g1b'], sem='g1b')
        NXB = 3

        def st_a(tt):
            b = tt % NXB
            tsl = slice(tt * 128, (tt + 1) * 128)
            P.dma('sp', lambda e, b=b, tsl=tsl: e.dma_start(out=xbuf[b], in_=x[tsl, :]), writes=[('xbuf', b)], sem=('x', b))
            P.act(lambda e, b=b, tt=tt: e.activation(out=junk, in_=xbuf[b], func=AF.Square, accum_out=ss1[:, tt:tt + 1]),
                  reads=[('xbuf', b)], writes=['junk', ('ss1', tt)])
            P.act(lambda e, tt=tt: e.activation(out=rt1[:, tt:tt + 1], in_=ss1[:, tt:tt + 1], func=AF.Sqrt, scale=1.0 / D, bias=EPS),
                  reads=[('ss1', tt)], writes=[('rt1', tt)])
            P.dve(lambda e, tt=tt: e.reciprocal(out=rs1[:, tt:tt + 1], in_=rt1[:, tt:tt + 1]), reads=[('rt1', tt)], writes=[('rs1', tt)])
            P.dve(lambda e, b=b, tt=tt: e.scalar_tensor_tensor(out=xnb[b], in0=xbuf[b], scalar=rs1[:, tt:tt + 1], in1=g1b, op0=ALU.mult, op1=ALU.mult),
                  reads=[('xbuf', b), ('rs1', tt), 'g1b'], writes=[('xnb', b)])

        pending = {}

        def st_b(tt):
            b = tt % NXB
            tsl = slice(tt * 128, (tt + 1) * 128)
            pt = 6 + (tt % 2)
            for kc in range(8):
                P.pe(lambda e, b=b, kc=kc, pt=pt: e.transpose(out=psb(pt)[:, kc, :], in_=xnb[b][:, kc * 128:(kc + 1) * 128], identity=ident_bf),
                     reads=[('xnb', b), 'ident_bf'], writes=['ps%d' % pt])
            P.act(lambda e, tsl=tsl, pt=pt: e.activation(out=xnT[:, :, tsl], in_=psb(pt), func=AF.Copy), reads=['ps%d' % pt], writes=[('xnT', tt)])
            for fn_ in pending.pop(tt, []):
                fn_()
            if tt % 4 == 3:
                tc = tt // 4
                csl = slice(tc * 512, (tc + 1) * 512)

                def u_mm(ct, tc=tc, csl=csl):
                    pb = 2 + (ct % 2)
                    for kc in range(8):
                        P.pe(lambda e, kc=kc, ct=ct, pb=pb, csl=csl: e.matmul(ps[pb], lhsT=Wu[:, kc, ct * 128:(ct + 1) * 128], rhs=xnT[:, kc, csl],
                                                                             start=(kc == 0), stop=(kc == 7)),
                             reads=[('xnT', t4) for t4 in range(tc * 4, tc * 4 + 4)] + ['Wu'], writes=['ps%d' % pb])

                def u_cp(ct, csl=csl):
                    pb = 2 + (ct % 2)
                    P.dve(lambda e, ct=ct, pb=pb, csl=csl: e.tensor_copy(out=uT[:, ct, csl], in_=ps[pb]), reads=['ps%d' % pb], writes=[('uT', ct)])
                u_mm(0)
                u_mm(1)
                pending.setdefault(tt + 1, []).extend([lambda: u_cp(0), lambda: u_cp(1), lambda: u_mm(2), lambda: u_mm(3)])
                pending.setdefault(tt + 2, []).extend([lambda: u_cp(2), lambda: u_cp(3)])

        st_a(0)
        st_a(1)
        for tt in range(NT):
            if tt + 2 < NT:
                st_a(tt + 2)
            st_b(tt)
            if hook is not None:
                hook(tt)
        for tt_ in sorted(pending):
            for fn_ in pending[tt_]:
                fn_()

    def phase_s5():
        Sa = Region(arena, RCm.start, RCm.end)
        Sb = Region(arena, RD_.start, SCR1)
        TWO_PI = 2.0 * math.pi

        def t16(n=1):
            return Sa.take([128, 16], F32) if n == 1 else Sa.take([128, n, 16], F32)
        AR, AI, LDT, DTt, MAG, ANG, ANG2, KF, R1, MSK, SIN, COS = [t16() for _ in range(12)]
        KI = Sa.take([128, 16], I32)
        ABR, ABI, DEN, RDEN, NR, COR, COI, T1, T2 = [t16() for _ in range(9)]
        AMr = [t16(), t16()]
        AMi = [t16(), t16()]
        PWR, PWI, NPWI = t16(2 * TM), t16(2 * TM), t16(2 * TM)
        PT1, PT2 = t16(2 * TM), t16(2 * TM)
        WKR, WKI = t16(TM), t16(TM)
        NS = int(round(math.log2(NM)))
        MUR, MUI, NMUI = t16(NS), t16(NS), t16(NS)
        BNr = Sa.take([128, 16, 32], F32)
        BNi = Sa.take([128, 16, 32], F32)
        CNr = Sb.take([32, 16, 128], BF16)
        CNi = Sb.take([32, 16, 128], BF16)
        CTr = Sa.take([128, 16, 32], F32)
        CTi = Sa.take([128, 16, 32], F32)
        CTr_bf = Sa.take([128, 16, 32], BF16)
        NCTi_bf = Sa.take([128, 16, 32], BF16)
        Dcol = Sa.take([128, 4], F32)
        diagD = Sa.take([128, 4, 128], BF16)
        bglu = Sa.take([128, 4], F32)
        Wglu = Sb.take([128, 4, 512], BF16)

        P.dma('sp', lambda e: e.dma_start(out=AR, in_=A_re.rearrange("(st g2) n -> (g2 n) st", g2=2), allow_slow_non_contiguous=True), writes=['AR'], sem='s5c', kind='group')
        P.dma('sp', lambda e: e.dma_start(out=AI, in_=A_im.rearrange("(st g2) n -> (g2 n) st", g2=2), allow_slow_non_contiguous=True), writes=['AI'], sem='s5c', kind='group')
        for g2 in range(2):
            P.dma('sp', lambda e, g2=g2: e.dma_start(out=LDT[64 * g2:64 * g2 + 64, :], in_=bass.AP(log_dt.tensor, g2, [[0, 64], [2, 16]]), allow_slow_non_contiguous=True),
                  writes=['LDT'], sem='s5c', kind='group')
        P.dma('sp', lambda e: e.dma_start(out=Dcol, in_=D_skip.rearrange("(c p) -> p c", p=128), allow_slow_non_contiguous=True), writes=['Dcol'], sem='s5c', kind='group')
        P.dma('sp', lambda e: e.dma_start(out=bglu, in_=b_glu.rearrange("(c p) -> p c", p=128), allow_slow_non_contiguous=True), writes=['bglu'], sem='s5c', kind='group')
        P.dve(lambda e: e.memset(BNr, 0.0), writes=['BNr'])
        P.dve(lambda e: e.memset(BNi, 0.0), writes=['BNi'])
        P.pool(lambda e: e.memset(CNr, 0.0), writes=['CNr'])
        P.pool(lambda e: e.memset(CNi, 0.0), writes=['CNi'])
        for g2 in range(2):
            for nm_, BN, Bd in (('BNr', BNr, B_re), ('BNi', BNi, B_im)):
                P.dma('sp', lambda e, g2=g2, BN=BN, Bd=Bd: e.dma_start(out=BN[64 * g2:64 * g2 + 64, :, 16 * g2:16 * g2 + 16],
                                                                        in_=bass.AP(Bd.tensor, g2 * 1024, [[16, 64], [2048, 16], [1, 16]])),
                      reads=[], writes=[nm_], sem='s5b', kind='group')
            for nm_, CN, Cd in (('CNr', CNr, C_re), ('CNi', CNi, C_im)):
                P.dma('pool', lambda e, g2=g2, CN=CN, Cd=Cd: e.dma_start(out=CN[16 * g2:16 * g2 + 16, :, 64 * g2:64 * g2 + 64],
                                                                          in_=bass.AP(Cd.tensor, g2 * 1024, [[64, 16], [2048, 16], [1, 64]])),
                      reads=[], writes=[nm_], sem='s5cn', kind='group')
        P.dma('pool', lambda e: e.dma_start(out=Wglu, in_=w_glu.rearrange("(kc p) n -> p kc n", p=128)), writes=['Wglu'], sem='wglu')

        def tt_(eng, out, a, b_, op, rk, wk):
            P.on(eng, lambda e: e.tensor_tensor(out=out, in0=a, in1=b_, op=op), reads=rk, writes=wk)

        def ts_(eng, out, a, s1, s2, op0, op1, rk, wk):
            if op1 is None:
                P.on(eng, lambda e: e.tensor_scalar(out=out, in0=a, scalar1=s1, scalar2=None, op0=op0), reads=rk, writes=wk)
            else:
                P.on(eng, lambda e: e.tensor_scalar(out=out, in0=a, scalar1=s1, scalar2=s2, op0=op0, op1=op1), reads=rk, writes=wk)

        P.act(lambda e: e.activation(out=DTt, in_=LDT, func=AF.Exp), reads=['LDT'], writes=['DT'])
        tt_('dve', MAG, AR, DTt, ALU.mult, ['AR', 'DT'], ['MAG'])
        P.act(lambda e: e.activation(out=MAG, in_=MAG, func=AF.Exp), reads=['MAG'], writes=['MAG'])
        tt_('dve', ANG, AI, DTt, ALU.mult, ['AI', 'DT'], ['ANG'])
        ts_('dve', ANG2, ANG, math.pi / 2, None, ALU.add, None, ['ANG'], ['ANG2'])

        def red_sin(out, ang, ko, ka):
            ts_('dve', KF, ang, 1.0 / TWO_PI, 0.5, ALU.mult, ALU.add, [ka], ['KF'])
            P.dve(lambda e: e.tensor_copy(out=KI, in_=KF), reads=['KF'], writes=['KI'])
            P.dve(lambda e: e.tensor_copy(out=KF, in_=KI), reads=['KI'], writes=['KF'])
            P.dve(lambda e: e.scalar_tensor_tensor(out=R1, in0=KF, scalar=-TWO_PI, in1=ang, op0=ALU.mult, op1=ALU.add), reads=['KF', ka], writes=['R1'])
            ts_('dve', MSK, R1, -math.pi, None, ALU.is_lt, None, ['R1'], ['MSK'])
            P.dve(lambda e: e.scalar_tensor_tensor(out=R1, in0=MSK, scalar=TWO_PI, in1=R1, op0=ALU.mult, op1=ALU.add), reads=['MSK', 'R1'], writes=['R1'])
            ts_('dve', MSK, R1, math.pi, None, ALU.is_gt, None, ['R1'], ['MSK'])
            P.dve(lambda e: e.scalar_tensor_tensor(out=R1, in0=MSK, scalar=-TWO_PI, in1=R1, op0=ALU.mult, op1=ALU.add), reads=['MSK', 'R1'], writes=['R1'])
            P.act(lambda e: e.activation(out=out, in_=R1, func=AF.Sin), reads=['R1'], writes=[ko])
        red_sin(SIN, ANG, 'SIN', 'ANG')
        red_sin(COS, ANG2, 'COS', 'ANG2')
        tt_('dve', ABR, MAG, COS, ALU.mult, ['MAG', 'COS'], ['ABR'])
        tt_('dve', ABI, MAG, SIN, ALU.mult, ['MAG', 'SIN'], ['ABI'])
        tt_('dve', DEN, AR, AR, ALU.mult, ['AR'], ['DEN'])
        tt_('dve', T1, AI, AI, ALU.mult, ['AI'], ['T1'])
        tt_('dve', DEN, DEN, T1, ALU.add, ['DEN', 'T1'], ['DEN'])
        P.dve(lambda e: e.reciprocal(out=RDEN, in_=DEN), reads=['DEN'], writes=['RDEN'])
        ts_('dve', NR, ABR, -1.0, None, ALU.add, None, ['ABR'], ['NR'])
        tt_('dve', T1, NR, AR, ALU.mult, ['NR', 'AR'], ['T1'])
        tt_('dve', T2, ABI, AI, ALU.mult, ['ABI', 'AI'], ['T2'])
        tt_('dve', T1, T1, T2, ALU.add, ['T1', 'T2'], ['T1'])
        tt_('dve', COR, T1, RDEN, ALU.mult, ['T1', 'RDEN'], ['COR'])
        tt_('dve', T1, ABI, AR, ALU.mult, ['ABI', 'AR'], ['T1'])
        tt_('dve', T2, NR, AI, ALU.mult, ['NR', 'AI'], ['T2'])
        tt_('dve', T1, T1, T2, ALU.subtract, ['T1', 'T2'], ['T1'])
        tt_('dve', COI, T1, RDEN, ALU.mult, ['T1', 'RDEN'], ['COI'])

        P.dve(lambda e: e.memset(PWR[:, 0, :], 1.0), writes=['PWR'])
        P.dve(lambda e: e.memset(PWI[:, 0, :], 0.0), writes=['PWI'])
        P.dve(lambda e: e.tensor_copy(out=AMr[0], in_=ABR), reads=['ABR'], writes=[('AMr', 0)])
        P.dve(lambda e: e.tensor_copy(out=AMi[0], in_=ABI), reads=['ABI'], writes=[('AMi', 0)])
        m = 1
        cur = 0
        while m < 2 * TM:
            bc = lambda t, m=m: t.unsqueeze(1).to_broadcast([128, m, 16])
            ar_, ai_ = AMr[cur], AMi[cur]
            kr, ki_ = ('AMr', cur), ('AMi', cur)
            tt_('dve', PT1[:, 0:m, :], PWR[:, 0:m, :], bc(ar_), ALU.mult, ['PWR', kr], ['PT1'])
            tt_('dve', PT2[:, 0:m, :], PWI[:, 0:m, :], bc(ai_), ALU.mult, ['PWI', ki_], ['PT2'])
            tt_('dve', PWR[:, m:2 * m, :], PT1[:, 0:m, :], PT2[:, 0:m, :], ALU.subtract, ['PT1', 'PT2', 'PWR'], ['PWR'])
            tt_('dve', PT1[:, 0:m, :], PWR[:, 0:m, :], bc(ai_), ALU.mult, ['PWR', ki_], ['PT1'])
            tt_('dve', PT2[:, 0:m, :], PWI[:, 0:m, :], bc(ar_), ALU.mult, ['PWI', kr], ['PT2'])
            tt_('dve', PWI[:, m:2 * m, :], PT1[:, 0:m, :], PT2[:, 0:m, :], ALU.add, ['PT1', 'PT2', 'PWI'], ['PWI'])
            nxt = 1 - cur
            tt_('dve', T1, ar_, ar_, ALU.mult, [kr], ['T1'])
            tt_('dve', T2, ai_, ai_, ALU.mult, [ki_], ['T2'])
            tt_('dve', AMr[nxt], T1, T2, ALU.subtract, ['T1', 'T2'], [('AMr', nxt)])
            tt_('dve', T1, ar_, ai_, ALU.mult, [kr, ki_], ['T1'])
            ts_('dve', AMi[nxt], T1, 2.0, None, ALU.mult, None, ['T1'], [('AMi', nxt)])
            cur = nxt
            m *= 2
        ts_('dve', NPWI, PWI, -1.0, None, ALU.mult, None, ['PWI'], ['NPWI'])
        bcc = lambda t: t.unsqueeze(1).to_broadcast([128, TM, 16])
        tt_('dve', PT1[:, 0:TM, :], PWR[:, 0:TM, :], bcc(COR), ALU.mult, ['PWR', 'COR'], ['PT1'])
        tt_('dve', PT2[:, 0:TM, :], PWI[:, 0:TM, :], bcc(COI), ALU.mult, ['PWI', 'COI'], ['PT2'])
        tt_('dve', WKR, PT1[:, 0:TM, :], PT2[:, 0:TM, :], ALU.subtract, ['PT1', 'PT2'], ['WKR'])
        tt_('dve', PT1[:, 0:TM, :], PWR[:, 0:TM, :], bcc(COI), ALU.mult, ['PWR', 'COI'], ['PT1'])
        tt_('dve', PT2[:, 0:TM, :], PWI[:, 0:TM, :], bcc(COR), ALU.mult, ['PWI', 'COR'], ['PT2'])
        tt_('dve', WKI, PT1[:, 0:TM, :], PT2[:, 0:TM, :], ALU.add, ['PT1', 'PT2'], ['WKI'])
        P.dve(lambda e: e.tensor_copy(out=MUR[:, 0, :], in_=PWR[:, TM, :]), reads=['PWR'], writes=['MUR'])
        P.dve(lambda e: e.tensor_copy(out=MUI[:, 0, :], in_=PWI[:, TM, :]), reads=['PWI'], writes=['MUI'])
        for s_ in range(NS - 1):
            tt_('dve', T1, MUR[:, s_, :], MUR[:, s_, :], ALU.mult, ['MUR'], ['T1'])
            tt_('dve', T2, MUI[:, s_, :], MUI[:, s_, :], ALU.mult, ['MUI'], ['T2'])
            tt_('dve', MUR[:, s_ + 1, :], T1, T2, ALU.subtract, ['T1', 'T2', 'MUR'], ['MUR'])
            tt_('dve', T1, MUR[:, s_, :], MUI[:, s_, :], ALU.mult, ['MUR', 'MUI'], ['T1'])
            ts_('dve', MUI[:, s_ + 1, :], T1, 2.0, None, ALU.mult, None, ['T1', 'MUI'], ['MUI'])

        ts_('dve', NMUI, MUI, -1.0, None, ALU.mult, None, ['MUI'], ['NMUI'])
        for c_ in range(4):
            P.dve(lambda e, c_=c_: e.tensor_scalar(out=diagD[:, c_, :], in0=ident_f, scalar1=Dcol[:, c_:c_ + 1], scalar2=None, op0=ALU.mult), reads=['ident_f', 'Dcol'], writes=['diagD'])
        for st in range(16):
            P.pe(lambda e, st=st: e.matmul(ps[0][:, st * 32:(st + 1) * 32], lhsT=CNr[0:32, st, :], rhs=ident_bf[0:32, 0:32], start=True, stop=True),
                 reads=['CNr', 'ident_bf'], writes=['ps0'])
            P.pe(lambda e, st=st: e.matmul(ps[1][:, st * 32:(st + 1) * 32], lhsT=CNi[0:32, st, :], rhs=ident_bf[0:32, 0:32], start=True, stop=True),
                 reads=['CNi', 'ident_bf'], writes=['ps1'])
        p6v = ps[0].rearrange("p (s c) -> p s c", s=16)
        p7v = ps[1].rearrange("p (s c) -> p s c", s=16)
        P.act(lambda e: e.activation(out=CTr, in_=p6v, func=AF.Copy), reads=['ps0'], writes=['CTr'])
        P.act(lambda e: e.activation(out=CTi, in_=p7v, func=AF.Copy), reads=['ps1'], writes=['CTi'])
        P.act(lambda e: e.activation(out=CTr_bf, in_=p6v, func=AF.Copy), reads=['ps0'], writes=['CTr_bf'])
        P.act(lambda e: e.activation(out=NCTi_bf, in_=p7v, func=AF.Copy, scale=-1.0), reads=['ps1'], writes=['NCTi_bf'])

        BtR = [Sb.take([128, TM, 32], BF16) for _ in range(8)]
        BtI = [Sb.take([128, TM, 32], BF16) for _ in range(8)]
        CMr = [Sb.take([128, TM, 32], BF16) for _ in range(8)]
        CMi = [Sb.take([128, TM, 32], BF16) for _ in range(8)]
        yield
        P.barrier()
        Sal = Region(arena, RD_.start, RD_.start + 8192)
        Ta = [Sal.take([128, TM, 32], F32) for _ in range(2)]
        Tb = [Sal.take([128, TM, 32], F32) for _ in range(2)]
        WBr2 = [Sb.take([128, TM, 128], BF16) for _ in range(2)]
        WBi2 = [Sb.take([128, TM, 128], BF16) for _ in range(2)]
        KT2 = [Sb.take([128, TM, 128], BF16) for _ in range(2)]
        SA = [Sb.take([128, 4, 2, NM + 1], F32) for _ in range(4)]
        Q1, Q3 = [Sb.take([128, 4, NM], F32) for _ in range(2)]
        Sbf2 = [Sb.take([128, 4, 2, NM + 1], BF16) for _ in range(2)]
        ysf = Sb.take([128, L], F32)
        G1 = [Sb.take([128, 512], F32) for _ in range(1)] * 2
        G2 = [Sb.take([128, 512], F32) for _ in range(1)] * 2
        uTv = uT.rearrange("p c (m j) -> p c j m", j=TM)

        def sa_keys(b_):
            return [('SA', b_, p_, x_) for p_ in 'ri' for x_ in (0, 1, 2, 3, 'h')]
        zrhs = ident4.rearrange("p a b -> p (a b)")
        for i_, t_ in enumerate(SA):
            P.dve(lambda e, t_=t_: e.memset(t_, 0.0), writes=sa_keys(i_))
        psW = [psall[:, 0:2048], psall[:, 2048:4096]]
        psS = psall[:, 2048:4096]
        psY = psall[:, 0:2048]
        def s5_prep(ct):
            bs_ = 4 * (ct % 2)
            for sp_ in range(4):
                st = 4 * ct + sp_
                e1, e2 = 'dve', 'pool'
                bB = lambda t, st=st: t[:, st, :].unsqueeze(1).to_broadcast([128, TM, 32])
                bW = lambda t, st=st: t[:, 0:TM, st].unsqueeze(2).to_broadcast([128, TM, 32])
                bP = lambda t, st=st: t[:, 1:TM + 1, st].unsqueeze(2).to_broadcast([128, TM, 32])
                tt_(e1, Ta[0], bB(BNr), bW(WKR), ALU.mult, ['BNr', 'WKR'], [('Ta', 0)])
                tt_(e1, Tb[0], bB(BNi), bW(WKI), ALU.mult, ['BNi', 'WKI'], [('Tb', 0)])
                tt_(e1, BtR[bs_ + sp_], Ta[0], Tb[0], ALU.subtract, [('Ta', 0), ('Tb', 0)], [('BtR', ct % 2, sp_)])
                tt_(e2, Ta[1], bB(BNr), bW(WKI), ALU.mult, ['BNr', 'WKI'], [('Ta', 1)])
                tt_(e2, Tb[1], bB(BNi), bW(WKR), ALU.mult, ['BNi', 'WKR'], [('Tb', 1)])
                tt_(e2, BtI[bs_ + sp_], Ta[1], Tb[1], ALU.add, [('Ta', 1), ('Tb', 1)], [('BtI', ct % 2, sp_)])
                tt_(e1, Ta[0], bB(CTr), bP(PWR), ALU.mult, ['CTr', 'PWR'], [('Ta', 0)])
                tt_(e1, Tb[0], bB(CTi), bP(PWI), ALU.mult, ['CTi', 'PWI'], [('Tb', 0)])
                tt_(e1, CMr[bs_ + sp_], Ta[0], Tb[0], ALU.subtract, [('Ta', 0), ('Tb', 0)], [('CMr', ct % 2, sp_)])
                e3 = 'dve'
                ti_ = 0 if e3 == 'dve' else 1
                tt_(e3, Ta[ti_], bB(CTr), bP(NPWI), ALU.mult, ['CTr', 'NPWI'], [('Ta', ti_)])
                tt_(e3, Tb[ti_], bB(CTi), bP(PWR), ALU.mult, ['CTi', 'PWR'], [('Tb', ti_)])
                tt_(e3, CMi[bs_ + sp_], Ta[ti_], Tb[ti_], ALU.subtract, [('Ta', ti_), ('Tb', ti_)], [('CMi', ct % 2, sp_)])
        def s5_main_a(ct):
            bs_ = 4 * (ct % 2)
            cb_ = ct % 2
            WBr, WBi, KT, Sbf = WBr2[cb_], WBi2[cb_], KT2[cb_], Sbf2[cb_]
            sb_ = 2 * cb_
            for ri, (Bt, WB, wk) in enumerate(((BtR, WBr, ('WBr', cb_)), (BtI, WBi, ('WBi', cb_)))):
                bkeys = ['ps%d' % (4 * ri + q) for q in range(4)]
                for i in range(TM):
                    k = TM - 1 - i
                    for sp_ in range(4):
                        P.pe(lambda e, ri=ri, i=i, k=k, sp_=sp_, Bt=Bt: e.matmul(psW[ri][32 * sp_:32 * sp_ + 32, i * 128:(i + 1) * 128], lhsT=Bt[bs_ + sp_][:, k, :], rhs=ident_bf,
                                                                              start=True, stop=True, tile_position=(0, 32 * sp_)),
                             reads=[('BtR' if ri == 0 else 'BtI', ct % 2, sp_), 'ident_bf'], writes=[bkeys[i // 4]])
                P.act(lambda e, ri=ri, WB=WB: e.activation(out=WB.rearrange("p a b -> p (a b)"), in_=psW[ri], func=AF.Copy), reads=bkeys, writes=[wk])
            for q in range(4):
                P.pe(lambda e, q=q: e.matmul(ps[q], lhsT=zeros_bf, rhs=zrhs, start=True, stop=False, skip_group_check=True),
                     reads=['zeros', 'ident4'], writes=['ps%d' % q])
            for tau in range(TM):
                for sp_ in range(4):
                    st = 4 * ct + sp_
                    o_ = psY[32 * sp_:32 * sp_ + 32, tau * 128 + 32 * sp_: tau * 128 + 32 * sp_ + 32]
                    P.pe(lambda e, o_=o_, tau=tau, sp_=sp_, st=st: e.matmul(o_, lhsT=BtR[bs_ + sp_][:, tau, :], rhs=CTr_bf[:, st, :], start=False, stop=False,
                                                                           tile_position=(0, 32 * sp_), skip_group_check=True),
                         reads=[('BtR', ct % 2, sp_), 'CTr_bf'], writes=['ps%d' % (tau // 4)])
                    P.pe(lambda e, o_=o_, tau=tau, sp_=sp_, st=st: e.matmul(o_, lhsT=BtI[bs_ + sp_][:, tau, :], rhs=NCTi_bf[:, st, :], start=False, stop=True,
                                                                           tile_position=(0, 32 * sp_), skip_group_check=True),
                         reads=[('BtI', ct % 2, sp_), 'NCTi_bf'], writes=['ps%d' % (tau // 4)])
            P.pe(lambda e, ct=ct: e.matmul(psY[:, 0:128], lhsT=diagD[:, ct, :], rhs=ident_bf, start=False, stop=True, skip_group_check=True),
                 reads=['diagD', 'ident_bf'], writes=['ps0'])
            P.act(lambda e: e.activation(out=KT.rearrange("p a b -> p (a b)"), in_=psY[:, 0:TM * 128], func=AF.Copy),
                  reads=['ps0', 'ps1', 'ps2', 'ps3'], writes=[('KT', cb_)])
            for sp_ in range(4):
                for ri, (WB, wk) in enumerate(((WBr, ('WBr', cb_)), (WBi, ('WBi', cb_)))):
                    reg = sp_ * 512 + ri * NM
                    for i in range(TM):
                        P.pe(lambda e, sp_=sp_, reg=reg, i=i, WB=WB, ct=ct: e.matmul(psS[:, reg:reg + NM], lhsT=WB[32 * sp_:32 * sp_ + 32, i, :], rhs=uTv[32 * sp_:32 * sp_ + 32, ct, i, :],
                                                                                    start=(i == 0), stop=(i == TM - 1), tile_position=(32 * sp_, 0)),
                             reads=[wk, ('uT', ct)], writes=['ps%d' % (4 + reg // 512)])
            ci_ = 0
            P.act(lambda e: e.activation(out=SA[sb_][:, :, :, 1:NM + 1],
                                         in_=psS.rearrange("p (a c) -> p a c", a=4)[:, :, 0:2 * NM].rearrange("p a (b c) -> p a b c", b=2), func=AF.Copy),
                  reads=['ps4', 'ps5', 'ps6', 'ps7'], writes=sa_keys(sb_))
        def s5_scan(ct):
            bs_ = 4 * (ct % 2)
            cb_ = ct % 2
            WBr, WBi, KT, Sbf = WBr2[cb_], WBi2[cb_], KT2[cb_], Sbf2[cb_]
            sb_ = 2 * cb_
            ci_ = 0
            for s_ in range(NS):
                d = 1 << s_
                n_ = NM - d
                cur, nxt = SA[sb_ + ci_], SA[sb_ + 1 - ci_]
                for sp_ in range(4):
                    st = 4 * ct + sp_
                    mr = MUR[:, s_, st:st + 1]
                    mi = MUI[:, s_, st:st + 1]
                    nmi = NMUI[:, s_, st:st + 1]
                    cr, ci = cur[:, sp_, 0, 1:NM + 1], cur[:, sp_, 1, 1:NM + 1]
                    nr_, ni_ = nxt[:, sp_, 0, 1:NM + 1], nxt[:, sp_, 1, 1:NM + 1]
                    q1, q3 = Q1[:, sp_, :], Q3[:, sp_, :]
                    kc_ = [('SA', sb_ + ci_, 'r', sp_), ('SA', sb_ + ci_, 'i', sp_), ('SA', sb_ + ci_, 'r', 'h'), ('SA', sb_ + ci_, 'i', 'h')]
                    P.dve(lambda e, q1=q1, cr=cr, mr=mr, n_=n_, d=d: e.scalar_tensor_tensor(out=q1[:, 0:n_], in0=cr[:, 0:n_], scalar=mr, in1=cr[:, d:NM], op0=ALU.mult, op1=ALU.add),
                          reads=kc_ + ['MUR'], writes=[('Q1', sp_)])
                    P.dve(lambda e, q1=q1, ci=ci, nmi=nmi, nr_=nr_, n_=n_, d=d: e.scalar_tensor_tensor(out=nr_[:, d:NM], in0=ci[:, 0:n_], scalar=nmi, in1=q1[:, 0:n_], op0=ALU.mult, op1=ALU.add),
                          reads=kc_ + ['NMUI', ('Q1', sp_)], writes=[('SA', sb_ + 1 - ci_, 'r', sp_)])
                    P.dve(lambda e, q3=q3, ci=ci, mr=mr, n_=n_, d=d: e.scalar_tensor_tensor(out=q3[:, 0:n_], in0=ci[:, 0:n_], scalar=mr, in1=ci[:, d:NM], op0=ALU.mult, op1=ALU.add),
                          reads=kc_ + ['MUR'], writes=[('Q3', sp_)])
                    P.dve(lambda e, q3=q3, cr=cr, mi=mi, ni_=ni_, n_=n_, d=d: e.scalar_tensor_tensor(out=ni_[:, d:NM], in0=cr[:, 0:n_], scalar=mi, in1=q3[:, 0:n_], op0=ALU.mult, op1=ALU.add),
                          reads=kc_ + ['MUI', ('Q3', sp_)], writes=[('SA', sb_ + 1 - ci_, 'i', sp_)])
                P.pool(lambda e, nxt=nxt, cur=cur, d=d: e.tensor_copy(out=nxt[:, :, :, 1:1 + d], in_=cur[:, :, :, 1:1 + d]), reads=sa_keys(sb_ + ci_), writes=[('SA', sb_ + 1 - ci_, 'r', 'h'), ('SA', sb_ + 1 - ci_, 'i', 'h')])
                ci_ = 1 - ci_
            kfin = sa_keys(sb_ + ci_)
            P.act(lambda e, fin=SA[sb_ + ci_]: e.activation(out=Sbf, in_=fin, func=AF.Copy), reads=kfin, writes=[('Sbf', cb_)])
        def s5_c(ct):
            bs_ = 4 * (ct % 2)
            cb_ = ct % 2
            WBr, WBi, KT, Sbf = WBr2[cb_], WBi2[cb_], KT2[cb_], Sbf2[cb_]
            sb_ = 2 * cb_
            for q in range(4):
                P.pe(lambda e, q=q: e.matmul(ps[q], lhsT=zeros_bf, rhs=zrhs, start=True, stop=False, skip_group_check=True),
                     reads=['zeros', 'ident4'], writes=['ps%d' % q])
            for j in range(TM):
                bk = 'ps%d' % (j * NM // 512)
                for i in range(j + 1):
                    P.pe(lambda e, j=j, i=i, ct=ct: e.matmul(psY[:, j * NM:(j + 1) * NM], lhsT=KT[:, j - i, :], rhs=uTv[:, ct, i, :], start=False, stop=False, skip_group_check=True),
                         reads=[('KT', cb_), ('uT', ct)], writes=[bk])
                for sp_ in range(4):
                    o_ = psY[32 * sp_:32 * sp_ + 32, j * NM:(j + 1) * NM]
                    P.pe(lambda e, o_=o_, j=j, sp_=sp_: e.matmul(o_, lhsT=CMr[bs_ + sp_][:, j, :], rhs=Sbf[:, sp_, 0, 0:NM], start=False, stop=False,
                                                                tile_position=(0, 32 * sp_), skip_group_check=True),
                         reads=[('CMr', ct % 2, sp_), ('Sbf', cb_)], writes=[bk])
                    P.pe(lambda e, o_=o_, j=j, sp_=sp_: e.matmul(o_, lhsT=CMi[bs_ + sp_][:, j, :], rhs=Sbf[:, sp_, 1, 0:NM], start=False, stop=True,
                                                                tile_position=(0, 32 * sp_), skip_group_check=True),
                         reads=[('CMi', ct % 2, sp_), ('Sbf', cb_)], writes=[bk])
            P.act(lambda e: e.activation(out=ysf.rearrange("p (m j) -> p j m", j=TM), in_=psY.rearrange("p (j m) -> p j m", j=TM), func=AF.Copy),
                  reads=['ps0', 'ps1', 'ps2', 'ps3'], writes=['ysf'])
            if ct == 0:
                dump("ysf0", ysf, [128, L])
        def s5_gelu(ct):
            bs_ = 4 * (ct % 2)
            cb_ = ct % 2
            WBr, WBi, KT, Sbf = WBr2[cb_], WBi2[cb_], KT2[cb_], Sbf2[cb_]
            sb_ = 2 * cb_
            for c4 in range(4):
                csl = slice(c4 * 512, (c4 + 1) * 512)
                gb = c4 % 2
                P.act(lambda e, csl=csl, gb=gb: e.activation(out=G1[gb], in_=ysf[:, csl], func=AF.Square), reads=['ysf'], writes=[('G1', 0)])
                ts_('dve', G1[gb], G1[gb], 0.044715, 1.0, ALU.mult, ALU.add, [('G1', 0)], [('G1', 0)])
                tt_('dve', G1[gb], G1[gb], ysf[:, csl], ALU.mult, [('G1', 0), 'ysf'], [('G1', 0)])
                P.act(lambda e, gb=gb: e.activation(out=G2[gb], in_=G1[gb], func=AF.Sigmoid, scale=2.0 * math.sqrt(2.0 / math.pi)), reads=[('G1', 0)], writes=[('G2', 0)])
                tt_('dve', ygT[:, ct, csl], G2[gb], ysf[:, csl], ALU.mult, [('G2', 0), 'ysf'], [('ygT', ct, c4)])
        s5_prep(0)
        s5_prep(1)
        s5_main_a(0)
        s5_main_a(1)
        s5_scan(0)
        s5_scan(1)
        s5_c(0)
        s5_prep(2)
        s5_gelu(0)
        s5_c(1)
        s5_prep(3)
        s5_gelu(1)
        s5_main_a(2)
        s5_main_a(3)
        s5_scan(2)
        s5_scan(3)
        s5_c(2)
        s5_gelu(2)
        s5_c(3)
        s5_gelu(3)
        dump("ygT", ygT, [128, 4, L])
        P.barrier()
        for oc in range(4):
            for c4 in range(4):
                csl = slice(c4 * 512, (c4 + 1) * 512)
                pb = 6 + (c4 % 2)
                for kc in range(4):
                    P.pe(lambda e, oc=oc, kc=kc, csl=csl, pb=pb: e.matmul(ps[pb], lhsT=Wglu[:, kc, oc * 128:(oc + 1) * 128], rhs=ygT[:, kc, csl], start=(kc == 0), stop=(kc == 3)),
                         reads=['Wglu'] + [('ygT', kc, c4)], writes=['ps%d' % pb])
                gb = c4 % 2
                P.act(lambda e, oc=oc, pb=pb, gb=gb: e.activation(out=G2[gb], in_=ps[pb], func=AF.Sigmoid, bias=bglu[:, oc:oc + 1]), reads=['ps%d' % pb, 'bglu'], writes=[('G2', 0)])
                tt_('dve', ysT[:, oc, csl], G2[gb], ygT[:, oc, csl], ALU.mult, [('G2', 0), ('ygT', oc, c4)], [('ysT', oc, c4)])
        dump("ysT", ysT, [128, 4, L])

    def phase_c():
        S = Region(arena, SCR0, SCR1)
        Wq = S.take([128, 8, 512], BF16)
        Wkv = S.take([128, 8, 452], BF16)
        gqk = S.take([128, 64], F32)
        gkb = S.take([128, 64], F32)
        gib = S.take([128, 64], F32)
        junk = S.take([128, 64], BF16)
        sq = S.take([128, 512], F32)
        ssq = S.take([128, NT, 10], F32)
        rtq = S.take([128, NT, 10], F32)
        rsq = S.take([128, NT, 10], F32)
        qh = [S.take([128, 8, 64], BF16) for _ in range(2)]
        kk = [S.take([128, 6, 64], BF16) for _ in range(2)]
        P.dma('pool', lambda e: e.dma_start(out=Wq, in_=w_in_v[:, :, 512:1024]), writes=['Wq'], sem='wq')
        P.dma('pool', lambda e: e.dma_start(out=Wkv, in_=w_in_v[:, :, 1024:1476]), writes=['Wkv'], sem='wkv')
        P.dma('sp', lambda e: e.dma_start(out=gqk, in_=q_norm_g.to_broadcast([128, 64])), writes=['gqk'], sem='c0', kind='group')
        P.dma('sp', lambda e: e.dma_start(out=gkb, in_=k_norm_g.to_broadcast([128, 64])), writes=['gkb'], sem='c0', kind='group')
        P.dma('sp', lambda e: e.dma_start(out=gib, in_=idx_k_norm_g.to_broadcast([128, 64])), writes=['gib'], sem='c0', kind='group')
        P.dve(lambda e: e.tensor_tensor(out=gqk, in0=gqk, in1=gkb, op=ALU.mult), reads=['gqk', 'gkb'], writes=['gqk'])
        P.pool(lambda e: e.memset(vaug[:, :, 64:65], 1.0), writes=['vaug1'])
        def c_a(tt):
            b = tt % 2
            tsl = slice(tt * 128, (tt + 1) * 128)
            pq, pk = 2 * b, 2 * b + 1
            kq, kk_ = 'ps%d' % pq, 'ps%d' % pk
            for kc in range(8):
                P.pe(lambda e, kc=kc, tsl=tsl, pq=pq: e.matmul(ps[pq], lhsT=xnT[:, kc, tsl], rhs=Wq[:, kc, :], start=(kc == 0), stop=(kc == 7)),
                     reads=[('xnT', tt), 'Wq'], writes=[kq])
            for kc in range(8):
                P.pe(lambda e, kc=kc, tsl=tsl, pk=pk: e.matmul(ps[pk][:, 0:452], lhsT=xnT[:, kc, tsl], rhs=Wkv[:, kc, :], start=(kc == 0), stop=(kc == 7)),
                     reads=[('xnT', tt), 'Wkv'], writes=[kk_])

        def c_a2(tt):
            b = tt % 2
            pq, pk = 2 * b, 2 * b + 1
            kq, kk_ = 'ps%d' % pq, 'ps%d' % pk
            P.act(lambda e, pq=pq: e.activation(out=sq, in_=ps[pq], func=AF.Square), reads=[kq], writes=['sq'])
            P.dve(lambda e, tt=tt: e.tensor_reduce(out=ssq[:, tt, 0:8], in_=sq.rearrange("p (h d) -> p h d", h=8), axis=AX.X, op=ALU.add),
                  reads=['sq'], writes=[('ssq', tt)])
            P.act(lambda e, tt=tt, pk=pk: e.activation(out=junk, in_=ps[pk][:, 0:64], func=AF.Square, accum_out=ssq[:, tt, 8:9]),
                  reads=[kk_], writes=['junk', ('ssq', tt)])
            P.act(lambda e, tt=tt, pk=pk: e.activation(out=junk, in_=ps[pk][:, 384:448], func=AF.Square, accum_out=ssq[:, tt, 9:10]),
                  reads=[kk_], writes=['junk', ('ssq', tt)])
            P.act(lambda e, tt=tt: e.activation(out=rtq[:, tt, :], in_=ssq[:, tt, :], func=AF.Sqrt, scale=1.0 / 64, bias=EPS),
                  reads=[('ssq', tt)], writes=[('rtq', tt)])
            P.dve(lambda e, tt=tt: e.reciprocal(out=rsq[:, tt, :], in_=rtq[:, tt, :]), reads=[('rtq', tt)], writes=[('rsq', tt)])
            P.dve(lambda e, tt=tt, b=b, pq=pq: e.tensor_tensor(out=qh[b], in0=ps[pq].rearrange("p (h d) -> p h d", h=8),
                                                              in1=rsq[:, tt, 0:8].unsqueeze(2).to_broadcast([128, 8, 64]), op=ALU.mult),
                  reads=[kq, ('rsq', tt)], writes=[('qh', b)])
            P.dve(lambda e, tt=tt, b=b, pk=pk: e.scalar_tensor_tensor(out=kk[b][:, 0, :], in0=ps[pk][:, 0:64], scalar=rsq[:, tt, 8:9], in1=gqk, op0=ALU.mult, op1=ALU.mult),
                  reads=[kk_, ('rsq', tt), 'gqk'], writes=[('kk', b)])
            P.dve(lambda e, tt=tt, b=b, pk=pk: e.scalar_tensor_tensor(out=kk[b][:, 1, :], in0=ps[pk][:, 384:448], scalar=rsq[:, tt, 9:10], in1=gib, op0=ALU.mult, op1=ALU.mult),
                  reads=[kk_, ('rsq', tt), 'gib'], writes=[('kk', b)])
            P.act(lambda e, b=b, pk=pk: e.activation(out=kk[b][:, 2:6, :], in_=ps[pk][:, 128:384].rearrange("p (h d) -> p h d", h=4), func=AF.Copy),
                  reads=[kk_], writes=[('kk', b)])
            P.act(lambda e, tt=tt, pk=pk: e.activation(out=vaug[:, tt, 0:64], in_=ps[pk][:, 64:128], func=AF.Copy), reads=[kk_], writes=[('vaug', tt)])
            P.act(lambda e, tt=tt, pk=pk: e.activation(out=wi_s[:, tt, :], in_=ps[pk][:, 448:452], func=AF.Copy, scale=W_SCALE), reads=[kk_], writes=[('wi', tt)])

        def c_b(tt):
            b = tt % 2
            tsl = slice(tt * 128, (tt + 1) * 128)
            p6, p5 = 4 + 2 * b, 5 + 2 * b
            for h in range(8):
                P.pe(lambda e, b=b, h=h, p6=p6: e.transpose(out=psb(p6)[0:64, h, :], in_=qh[b][:, h, :], identity=ident_bf),
                     reads=[('qh', b), 'ident_bf'], writes=['ps%d' % p6])
            P.act(lambda e, tsl=tsl, p6=p6: e.activation(out=qT[0:64, :, tsl], in_=psb(p6)[0:64], func=AF.Copy), reads=['ps%d' % p6], writes=[('qT', tt)])
            for j in range(6):
                P.pe(lambda e, b=b, j=j, p5=p5: e.transpose(out=psb(p5)[0:64, j, :], in_=kk[b][:, j, :], identity=ident_bf),
                     reads=[('kk', b), 'ident_bf'], writes=['ps%d' % p5])
            P.dve(lambda e, tsl=tsl, p5=p5: e.tensor_copy(out=kT[:, tsl], in_=psb(p5)[0:64, 0, :]), reads=['ps%d' % p5], writes=[('kT', tt)])
            P.dve(lambda e, tsl=tsl, p5=p5: e.tensor_copy(out=kiT[:, tsl], in_=psb(p5)[0:64, 1, :]), reads=['ps%d' % p5], writes=[('kiT', tt)])
            P.dve(lambda e, tsl=tsl, p5=p5: e.tensor_copy(out=qiT[:, :, tsl], in_=psb(p5)[0:64, 2:6, :]), reads=['ps%d' % p5], writes=[('qiT', tt)])

        c_a(0)
        c_a(1)
        c_a2(0)
        for tt in range(NT):
            if tt + 2 < NT:
                c_a(tt + 2)
            if tt + 1 < NT:
                c_a2(tt + 1)
            c_b(tt)

    def phase_attn():
        S = Region(arena, SCR0, SCR1)
        G = 2
        isc = [[S.take([128, L], F32) for _ in range(G)] for _ in range(2)]
        bias = [[S.take([128, L], BF16) for _ in range(G)] for _ in range(2)]
        cum = S.take([128, L], F32)
        gtb = S.take([128, L], BF16)
        rr = [S.take([128, 512], F32) for _ in range(4)]
        tmpi = S.take([128, 512], F32)
        PT = [S.take([128, 1024], BF16) for _ in range(2)]
        rden = S.take([128, 8], F32)
        lnd = S.take([128, 8], F32)
        oh = [S.take([128, 8, 64], BF16) for _ in range(2)]
        NB_ = 31
        BITS = S.take([128, NB_], I32)
        n0 = S.take([128, G], F32)
        sig_f = S.take([128, G], F32)
        Pk = S.take([128, G], I32)
        cand = S.take([128, G], I32)
        uu = S.take([128, G], I32)
        C31 = S.take([128, 2], I32)
        cnt = S.take([128, G], F32)
        dd = S.take([128, G], F32)
        ngt = S.take([128, G], F32)
        rsel = S.take([128, G], F32)
        ps_s2 = [psall[:, 4 * 512:6 * 512], psall[:, 2 * 512:4 * 512]]
        zrhs = ident4.rearrange("p a b -> p (a b)")
        KSEL = 256
        for j in range(NB_):
            bb = 30 - j
            P.pool(lambda e, j=j, bb=bb: e.memset(BITS[:, j:j + 1], 1 << bb), writes=['BITS'])
        P.pool(lambda e: e.memset(C31, 31), writes=['C31'])

        def tile_of(k, g):
            return 2 + 2 * k + g
        GORDER = [0, 6, 5, 4, 3, 2, 1]
        pos_of = {k_: i_ for i_, k_ in enumerate(GORDER)}

        def idx(qt):
            k, g = (qt - 2) // 2, (qt - 2) % 2
            gp = pos_of[k] % 2
            qsl = slice(qt * 128, (qt + 1) * 128)
            Sv = 128 * (qt + 1)
            iscb = isc[gp][g]
            kI = ('isc', gp, g)
            for c in range((Sv + 511) // 512):
                c0, c1 = c * 512, min(Sv, (c + 1) * 512)
                w = c1 - c0
                for h in range(4):
                    P.pe(lambda e, h=h, c0=c0, c1=c1, w=w, qsl=qsl: e.matmul(ps[h][:, 0:w], lhsT=qiT[:, h, qsl], rhs=kiT[:, c0:c1], start=True, stop=True),
                         reads=[('qiT', qt)] + [('kiT', t) for t in range(c0 // 128, c1 // 128)], writes=['ps%d' % h])
                    P.act(lambda e, h=h, w=w: e.activation(out=rr[h][:, 0:w], in_=ps[h][:, 0:w], func=AF.Relu), reads=['ps%d' % h], writes=[('rr', h)])
                    if h == 0:
                        P.pool(lambda e, c0=c0, c1=c1, w=w, qt=qt, iscb=iscb: e.tensor_scalar(out=iscb[:, c0:c1], in0=rr[0][:, 0:w], scalar1=wi_s[:, qt, 0:1], scalar2=0.0, op0=ALU.mult, op1=ALU.add),
                               reads=[('rr', 0), ('wi', qt)], writes=[kI])
                    else:
                        P.pool(lambda e, h=h, w=w, qt=qt: e.tensor_scalar(out=tmpi[:, 0:w], in0=rr[h][:, 0:w], scalar1=wi_s[:, qt, h:h + 1], scalar2=0.0, op0=ALU.mult, op1=ALU.add),
                               reads=[('rr', h), ('wi', qt)], writes=['tmpi'])
                        P.pool(lambda e, c0=c0, c1=c1, w=w, iscb=iscb: e.tensor_tensor(out=iscb[:, c0:c1], in0=iscb[:, c0:c1], in1=tmpi[:, 0:w], op=ALU.add),
                               reads=['tmpi', kI], writes=[kI])
            P.pool(lambda e, qsl=qsl, iscb=iscb: e.tensor_tensor(out=iscb[:, qsl], in0=iscb[:, qsl], in1=cneg, op=ALU.add), reads=[kI, 'cneg'], writes=[kI])

        def topk_group(k):
            gp = pos_of[k] % 2
            tiles = [tile_of(k, g) for g in range(G)]
            Svs = [128 * (qt + 1) for qt in tiles]
            kIs = [('isc', gp, g) for g in range(G)]
            for g in range(G):
                P.dve(lambda e, g=g: e.tensor_scalar(out=bias[gp][g][:, 0:Svs[g]], in0=isc[gp][g][:, 0:Svs[g]], scalar1=0.0, scalar2=None, op0=ALU.is_ge, op1=ALU.add, accum_out=n0[:, g:g + 1]),
                      reads=[kIs[g]], writes=[('n0', g), ('bias', gp, g)])
            P.dve(lambda e: e.tensor_single_scalar(out=sig_f, in_=n0, scalar=float(KSEL), op=ALU.is_lt), reads=[('n0', g) for g in range(G)], writes=['sig_f'])
            P.dve(lambda e: e.tensor_single_scalar(out=dd, in_=sig_f, scalar=-1.0, op=ALU.mult), reads=['sig_f'], writes=['dd'])
            P.dve(lambda e: e.tensor_copy(out=Pk, in_=dd), reads=['dd'], writes=['Pk'])
            P.dve(lambda e: e.tensor_scalar(out=cand, in0=Pk, scalar1=BITS[:, 0:1], scalar2=None, op0=ALU.bitwise_xor), reads=['Pk', 'BITS'], writes=['cand'])
            for j in range(NB_):
                for g in range(G):
                    P.dve(lambda e, g=g: e.tensor_scalar(out=bias[gp][g][:, 0:Svs[g]], in0=isc[gp][g][:, 0:Svs[g]], scalar1=cand[:, g:g + 1].bitcast(F32), scalar2=0.5 - KSEL,
                                                         op0=ALU.is_ge, op1=ALU.add, accum_out=cnt[:, g:g + 1]),
                          reads=[kIs[g], 'cand'], writes=[('cnt', g), ('bias', gp, g)])
                P.dve(lambda e, j=j: e.tensor_scalar(out=uu, in0=cnt.bitcast(I32), scalar1=C31[:, 0:1], scalar2=BITS[:, j:j + 1], op0=ALU.arith_shift_right, op1=ALU.bitwise_and),
                      reads=[('cnt', g) for g in range(G)] + ['BITS', 'C31'], writes=['uu'])
                if j + 1 < NB_:
                    P.dve(lambda e, j=j: e.scalar_tensor_tensor(out=cand, in0=uu, scalar=BITS[:, j + 1:j + 2], in1=cand, op0=ALU.bitwise_xor, op1=ALU.bitwise_xor),
                          reads=['uu', 'BITS', 'cand'], writes=['cand'])
                else:
                    P.dve(lambda e: e.tensor_tensor(out=Pk, in0=uu, in1=cand, op=ALU.bitwise_xor), reads=['uu', 'cand'], writes=['Pk'])
            for g in range(G):
                Sv = Svs[g]
                iscb, biasb = isc[gp][g], bias[gp][g]
                kI, kB = kIs[g], ('bias', gp, g)
                thr = Pk[:, g:g + 1].bitcast(F32)
                P.dve(lambda e, Sv=Sv, iscb=iscb, thr=thr, g=g: e.tensor_scalar(out=gtb[:, 0:Sv], in0=iscb[:, 0:Sv], scalar1=thr, scalar2=None, op0=ALU.is_gt, op1=ALU.add, accum_out=ngt[:, g:g + 1]),
                      reads=[kI, 'Pk', 'gtb', 'cum'], writes=['gtb', 'ngt'])
                P.dve(lambda e, g=g: e.tensor_scalar(out=rsel[:, g:g + 1], in0=ngt[:, g:g + 1], scalar1=-1.0, scalar2=float(KSEL), op0=ALU.mult, op1=ALU.add), reads=['ngt'], writes=['rsel'])
                P.dve(lambda e, Sv=Sv, iscb=iscb, thr=thr: e.tensor_scalar(out=iscb[:, 0:Sv], in0=iscb[:, 0:Sv], scalar1=thr, scalar2=None, op0=ALU.is_equal),
                      reads=[kI, 'Pk'], writes=[kI])
                P.dve(lambda e, Sv=Sv, iscb=iscb: e.tensor_tensor_scan(out=cum[:, 0:Sv], data0=iscb[:, 0:Sv], data1=iscb[:, 0:Sv], initial=0.0, op0=ALU.add, op1=ALU.max),
                      reads=[kI], writes=['cum'])
                P.dve(lambda e, Sv=Sv, iscb=iscb, g=g: e.scalar_tensor_tensor(out=cum[:, 0:Sv], in0=cum[:, 0:Sv], scalar=rsel[:, g:g + 1], in1=iscb[:, 0:Sv], op0=ALU.is_le, op1=ALU.mult),
                      reads=['cum', 'rsel', kI], writes=['cum'])
                P.dve(lambda e, Sv=Sv: e.tensor_tensor(out=cum[:, 0:Sv], in0=cum[:, 0:Sv], in1=gtb[:, 0:Sv], op=ALU.add), reads=['cum', 'gtb'], writes=['cum'])
                P.dve(lambda e, Sv=Sv, biasb=biasb: e.tensor_scalar(out=biasb[:, 0:Sv], in0=cum[:, 0:Sv], scalar1=-NEG, scalar2=NEG, op0=ALU.mult, op1=ALU.add),
                      reads=['cum'], writes=[kB])

        def attn(qt):
            b = qt % 2
            qsl = slice(qt * 128, (qt + 1) * 128)
            if qt >= 2:
                k_, g_ = (qt - 2) // 2, (qt - 2) % 2
                biasb = bias[pos_of[k_] % 2][g_]
                kB = ('bias', pos_of[k_] % 2, g_)
            else:
                biasb, kB = None, None
            P.pe(lambda e: e.matmul(ps[6], lhsT=zeros_bf, rhs=zrhs, start=True, stop=False, skip_group_check=True), reads=['zeros', 'ident4'], writes=['ps6'])
            P.pe(lambda e: e.matmul(ps[7], lhsT=zeros_bf, rhs=zrhs, start=True, stop=False, skip_group_check=True), reads=['zeros', 'ident4'], writes=['ps7'])
            for kt in range(qt + 1):
                ksl = slice(kt * 128, (kt + 1) * 128)
                pb = kt % 2
                if qt >= 2:
                    bsrc, bkeys = biasb[:, ksl], [kB]
                elif kt == qt:
                    bsrc, bkeys = cbias, ['cbias']
                else:
                    bsrc, bkeys = None, []
                ps_s = ps_s2[kt % 2]
                sb0 = 4 if kt % 2 == 0 else 2
                for half in range(2):
                    P.pe(lambda e, half=half, ksl=ksl, qsl=qsl, ps_s=ps_s, last=(bsrc is None): e.matmul(ps_s[:, half * 512:(half + 1) * 512], lhsT=kT[:, ksl], rhs=qT[0:64, 4 * half:4 * half + 4, qsl],
                                                                                                       start=True, stop=last),
                         reads=[('kT', kt), ('qT', qt)], writes=['ps%d' % (sb0 + half)])
                    if bsrc is not None:
                        P.pe(lambda e, half=half, bsrc=bsrc, ps_s=ps_s: e.matmul(ps_s[:, half * 512:(half + 1) * 512], lhsT=bsrc, rhs=ident4, start=False, stop=True),
                             reads=bkeys + ['ident4'], writes=['ps%d' % (sb0 + half)])
                P.act(lambda e, pb=pb, ps_s=ps_s: e.activation(out=PT[pb], in_=ps_s, func=AF.Exp, scale=0.125), reads=['ps%d' % sb0, 'ps%d' % (sb0 + 1)], writes=[('PT', pb)])
                for h in range(8):
                    P.pe(lambda e, h=h, pb=pb, kt=kt, last=(kt == qt): e.matmul(ps[6 + h // 4][:, (h % 4) * 65:(h % 4) * 65 + 65], lhsT=PT[pb][:, h * 128:(h + 1) * 128], rhs=vaug[:, kt, :],
                                                                               start=False, stop=last, skip_group_check=True),
                         reads=[('PT', pb), ('vaug', kt), 'vaug1'], writes=['ps%d' % (6 + h // 4)])
            for hb in range(2):
                pv = ps[6 + hb][:, 0:260].rearrange("p (h d) -> p h d", h=4)
                P.act(lambda e, hb=hb, pv=pv: e.activation(out=lnd[:, 4 * hb:4 * hb + 4], in_=pv[:, :, 64], func=AF.Ln), reads=['ps%d' % (6 + hb)], writes=[('lnd', hb)])
                P.act(lambda e, hb=hb: e.activation(out=rden[:, 4 * hb:4 * hb + 4], in_=lnd[:, 4 * hb:4 * hb + 4], func=AF.Exp, scale=-1.0), reads=[('lnd', hb)], writes=[('rden', hb)])
            for h in range(8):
                P.act(lambda e, h=h, b=b: e.activation(out=oh[b][:, h, :], in_=ps[6 + h // 4][:, (h % 4) * 65:(h % 4) * 65 + 64], func=AF.Copy, scale=rden[:, h:h + 1]),
                      reads=['ps%d' % (6 + h // 4), ('rden', h // 4)], writes=[('oh', b)])
            for c in range(4):
                P.pe(lambda e, c=c, b=b: e.transpose(out=psb(0)[:, c, :], in_=oh[b].rearrange("p h d -> p (h d)")[:, c * 128:(c + 1) * 128], identity=ident_bf),
                     reads=[('oh', b), 'ident_bf'], writes=['ps0'])
            P.act(lambda e, qsl=qsl: e.activation(out=yaT[:, :, qsl], in_=psb(0)[:, 0:4, :], func=AF.Copy), reads=['ps0'], writes=[('yaT', qt)])

        NG = (NT - 2) // G
        assert sorted(GORDER) == list(range(NG))
        idx(tile_of(GORDER[0], 0))
        idx(tile_of(GORDER[0], 1))
        topk_group(GORDER[0])
        attn(0)
        attn(1)
        for p_ in range(NG):
            if p_ + 1 < NG:
                kn = GORDER[p_ + 1]
                idx(tile_of(kn, 0))
                idx(tile_of(kn, 1))
                topk_group(kn)
            attn(tile_of(GORDER[p_], 0))
            attn(tile_of(GORDER[p_], 1))

    s5gen = phase_s5()
    P.defer()
    next(s5gen)
    P.pause_defer()
    nq = len(P.deferq)
    per = (nq + NT - 3) // (NT - 2)
    phase_ab(hook=lambda tt: P.flush(per) if tt >= 1 else None)
    P.flush()
    dump("xnT", xnT, [128, 8, L])
    dump("uT", uT, [128, 4, L])
    P.barrier()
    for _ in s5gen:
        pass
    P.barrier()
    if 'stop_s5' not in dbg:
        phase_c()
        P.barrier()
        phase_attn()
        dump("yaT", yaT, [128, 4, L])

    def phase_merge():
        S = Region(arena, RD_.start, SCR1)
        Wps = S.take([128, 4, D], BF16)
        Wpa = S.take([128, 4, D], BF16)
        Wg1 = S.take([128, 8, D], BF16)
        Wg2 = S.take([128, 8, D], BF16)
        sg = [S.take([128, 512], F32) for _ in range(4)]
        mm = [S.take([128, 512], F32) for _ in range(4)]
        wpsv = w_proj_ssm.rearrange("(kc p) n -> p kc n", p=128)
        wpav = w_proj_attn.rearrange("(kc p) n -> p kc n", p=128)
        for oc in range(8):
            osl = slice(oc * 128, (oc + 1) * 128)
            P.dma('pool', lambda e, osl=osl: e.dma_start(out=Wps[:, :, osl], in_=wpsv[:, :, osl]), writes=[('Wps', oc)], sem=('wps', oc))
            P.dma('pool', lambda e, osl=osl, oc=oc: e.dma_start(out=Wg1[:, :, osl], in_=w_in_v[:, :, 1476 + oc * 128:1476 + (oc + 1) * 128]), writes=[('Wg1', oc)], sem=('wg1', oc))
            P.dma('pool', lambda e, osl=osl: e.dma_start(out=Wpa[:, :, osl], in_=wpav[:, :, osl]), writes=[('Wpa', oc)], sem=('wpa', oc))
            P.dma('pool', lambda e, osl=osl, oc=oc: e.dma_start(out=Wg2[:, :, osl], in_=w_in_v[:, :, 2500 + oc * 128:2500 + (oc + 1) * 128]), writes=[('Wg2', oc)], sem=('wg2', oc))
        it = 0
        for tc in range(4):
            csl = slice(tc * 512, (tc + 1) * 512)
            xk = [('xnT', t4) for t4 in range(tc * 4, tc * 4 + 4)]
            yak = [('yaT', t4) for t4 in range(tc * 4, tc * 4 + 4)]
            for oc in range(8):
                s_ = it % 2
                it += 1
                osl = slice(oc * 128, (oc + 1) * 128)
                ba, bb, bc_, bd = [4 * s_ + q for q in range(4)]
                for k in range(4):
                    P.pe(lambda e, k=k, osl=osl, csl=csl, ba=ba: e.matmul(ps[ba], lhsT=Wps[:, k, osl], rhs=ysT[:, k, csl], start=(k == 0), stop=(k == 3)),
                         reads=[('Wps', oc)] + [('ysT', k, tc)], writes=['ps%d' % ba])
                for k in range(8):
                    P.pe(lambda e, k=k, osl=osl, csl=csl, bb=bb: e.matmul(ps[bb], lhsT=Wg1[:, k, osl], rhs=xnT[:, k, csl], start=(k == 0), stop=(k == 7)),
                         reads=[('Wg1', oc)] + xk, writes=['ps%d' % bb])
                for k in range(4):
                    P.pe(lambda e, k=k, osl=osl, csl=csl, bc_=bc_: e.matmul(ps[bc_], lhsT=Wpa[:, k, osl], rhs=yaT[:, k, csl], start=(k == 0), stop=(k == 3)),
                         reads=[('Wpa', oc)] + yak, writes=['ps%d' % bc_])
                for k in range(8):
                    P.pe(lambda e, k=k, osl=osl, csl=csl, bd=bd: e.matmul(ps[bd], lhsT=Wg2[:, k, osl], rhs=xnT[:, k, csl], start=(k == 0), stop=(k == 7)),
                         reads=[('Wg2', oc)] + xk, writes=['ps%d' % bd])
                P.act(lambda e, s_=s_, bb=bb: e.activation(out=sg[2 * s_], in_=ps[bb], func=AF.Sigmoid), reads=['ps%d' % bb], writes=[('sg', 2 * s_)])
                P.act(lambda e, s_=s_, bd=bd: e.activation(out=sg[2 * s_ + 1], in_=ps[bd], func=AF.Sigmoid), reads=['ps%d' % bd], writes=[('sg', 2 * s_ + 1)])
                P.dve(lambda e, s_=s_, ba=ba: e.tensor_tensor(out=mm[2 * s_], in0=sg[2 * s_], in1=ps[ba], op=ALU.mult), reads=[('sg', 2 * s_), 'ps%d' % ba], writes=[('mm', 2 * s_)])
                P.dve(lambda e, s_=s_, bc_=bc_: e.tensor_tensor(out=mm[2 * s_ + 1], in0=sg[2 * s_ + 1], in1=ps[bc_], op=ALU.mult), reads=[('sg', 2 * s_ + 1), 'ps%d' % bc_], writes=[('mm', 2 * s_ + 1)])
                P.dve(lambda e, s_=s_, oc=oc, csl=csl: e.tensor_tensor(out=mergedT[:, oc, csl], in0=mm[2 * s_], in1=mm[2 * s_ + 1], op=ALU.add),
                      reads=[('mm', 2 * s_), ('mm', 2 * s_ + 1)], writes=[('mergedT', oc, tc)])

    X1_OFF = 139264
    x1 = Region(arena, X1_OFF, ARENA_BYTES).take([128, NT, D], F32)
    xn2T = xnT

    def phase_out():
        S = Region(arena, RA_.start, X1_OFF)
        Wout = S.take([128, 8, D], BF16)
        g2b = S.take([128, D], F32)
        xbuf = [S.take([128, D], F32) for _ in range(2)]
        xnb = [S.take([128, D], BF16) for _ in range(2)]
        junk = S.take([128, D], BF16)
        ss2 = S.take([128, NT], F32)
        rt2 = S.take([128, NT], F32)
        rs2 = S.take([128, NT], F32)
        woutv = w_out.rearrange("(kc p) n -> p kc n", p=128)
        for nh in range(2):
            P.dma('pool', lambda e, nh=nh: e.dma_start(out=Wout[:, :, nh * 512:(nh + 1) * 512], in_=woutv[:, :, nh * 512:(nh + 1) * 512]), writes=[('Wout', nh)], sem=('wout', nh))
        P.dma('sp', lambda e: e.dma_start(out=g2b, in_=norm2_g.to_broadcast([128, D])), writes=['g2b'], sem='g2b')
        def g_a(tt):
            b = tt % 2
            tsl = slice(tt * 128, (tt + 1) * 128)
            pq = 2 * (tt % 2)
            pso = psall[:, pq * 512:(pq + 2) * 512]
            P.dma('sp', lambda e, b=b, tsl=tsl: e.dma_start(out=xbuf[b], in_=x[tsl, :]), writes=[('xbuf2', b)], sem=('x2', b))
            for nh in range(2):
                for kc in range(8):
                    P.pe(lambda e, kc=kc, nh=nh, tsl=tsl, pq=pq: e.matmul(ps[pq + nh], lhsT=mergedT[:, kc, tsl], rhs=Wout[:, kc, nh * 512:(nh + 1) * 512], start=(kc == 0), stop=(kc == 7)),
                         reads=[('mergedT', kc, tt // 4), ('Wout', nh)], writes=['ps%d' % (pq + nh)])
            P.dve(lambda e, tt=tt, b=b, pso=pso: e.tensor_tensor(out=x1[:, tt, :], in0=pso, in1=xbuf[b], op=ALU.add),
                  reads=['ps%d' % pq, 'ps%d' % (pq + 1), ('xbuf2', b)], writes=[('x1', tt)])
            P.act(lambda e, tt=tt: e.activation(out=junk, in_=x1[:, tt, :], func=AF.Square, accum_out=ss2[:, tt:tt + 1]),
                  reads=[('x1', tt)], writes=['junk2', ('ss2', tt)])
            P.act(lambda e, tt=tt: e.activation(out=rt2[:, tt:tt + 1], in_=ss2[:, tt:tt + 1], func=AF.Sqrt, scale=1.0 / D, bias=EPS),
                  reads=[('ss2', tt)], writes=[('rt2', tt)])
            P.dve(lambda e, tt=tt: e.reciprocal(out=rs2[:, tt:tt + 1], in_=rt2[:, tt:tt + 1]), reads=[('rt2', tt)], writes=[('rs2', tt)])
            P.dve(lambda e, b=b, tt=tt: e.scalar_tensor_tensor(out=xnb[b], in0=x1[:, tt, :], scalar=rs2[:, tt:tt + 1], in1=g2b, op0=ALU.mult, op1=ALU.mult),
                  reads=[('x1', tt), ('rs2', tt), 'g2b'], writes=[('xnb2', b)])

        def g_b(tt):
            b = tt % 2
            tsl = slice(tt * 128, (tt + 1) * 128)
            pt = 6 + b
            for kc in range(8):
                P.pe(lambda e, b=b, kc=kc, pt=pt: e.transpose(out=psb(pt)[:, kc, :], in_=xnb[b][:, kc * 128:(kc + 1) * 128], identity=ident_bf),
                     reads=[('xnb2', b), 'ident_bf'], writes=['ps%d' % pt])
            P.act(lambda e, tsl=tsl, pt=pt: e.activation(out=xn2T[:, :, tsl], in_=psb(pt), func=AF.Copy), reads=['ps%d' % pt], writes=[('xn2T', tt)])

        g_a(0)
        for tt in range(NT):
            if tt + 1 < NT:
                g_a(tt + 1)
            g_b(tt)

    def phase_ffn():
        S = Region(arena, RCm.start, X1_OFF)
        hT = S.take([128, NFC, 1024], BF16)
        Wd = [S.take([128, NFC, 512], BF16) for _ in range(2)]
        Wg = [S.take([128, 8, 128], BF16) for _ in range(2)]
        Wu = [S.take([128, 8, 128], BF16) for _ in range(2)]
        sgl = [S.take([128, 512], F32) for _ in range(2)]
        obuf = sgl
        wgv = w_ffn_gate.rearrange("(kc p) n -> p kc n", p=128)
        wuv = w_ffn_up.rearrange("(kc p) n -> p kc n", p=128)
        wdv = w_ffn_down.rearrange("(fc p) n -> p fc n", p=128)
        it = 0
        oi = 0
        for half in range(2):
            for fc in range(NFC):
                wb = fc % 2
                fsl = slice(fc * 128, (fc + 1) * 128)
                P.dma('pool', lambda e, wb=wb, fsl=fsl: e.dma_start(out=Wg[wb], in_=wgv[:, :, fsl]), writes=[('Wg', wb)], sem=('wg', wb))
                P.dma('pool', lambda e, wb=wb, fsl=fsl: e.dma_start(out=Wu[wb], in_=wuv[:, :, fsl]), writes=[('Wu', wb)], sem=('wu', wb))
                if fc in (8, 16):
                    nh_ = 0 if fc == 8 else 1
                    P.dma('pool', lambda e, nh_=nh_: e.dma_start(out=Wd[nh_], in_=wdv[:, :, nh_ * 512:(nh_ + 1) * 512]), writes=[('Wd', nh_)], sem=('wd', nh_))
                for tc2 in range(2):
                    c0 = half * 1024 + tc2 * 512
                    csl = slice(c0, c0 + 512)
                    xk = [('xn2T', t4) for t4 in range(c0 // 128, c0 // 128 + 4)]
                    s_ = it % 2
                    it += 1
                    bg, bu = 2 * s_, 2 * s_ + 1
                    for kc in range(8):
                        P.pe(lambda e, kc=kc, wb=wb, csl=csl, bg=bg: e.matmul(ps[bg], lhsT=Wg[wb][:, kc, :], rhs=xn2T[:, kc, csl], start=(kc == 0), stop=(kc == 7)),
                             reads=[('Wg', wb)] + xk, writes=['ps%d' % bg])
                    for kc in range(8):
                        P.pe(lambda e, kc=kc, wb=wb, csl=csl, bu=bu: e.matmul(ps[bu], lhsT=Wu[wb][:, kc, :], rhs=xn2T[:, kc, csl], start=(kc == 0), stop=(kc == 7)),
                             reads=[('Wu', wb)] + xk, writes=['ps%d' % bu])
                    P.act(lambda e, s_=s_, bg=bg: e.activation(out=sgl[s_], in_=ps[bg], func=AF.Silu), reads=['ps%d' % bg], writes=[('sgl', s_)])
                    P.dve(lambda e, s_=s_, bu=bu, fc=fc, tc2=tc2: e.tensor_tensor(out=hT[:, fc, tc2 * 512:(tc2 + 1) * 512], in0=sgl[s_], in1=ps[bu], op=ALU.mult),
                          reads=[('sgl', s_), 'ps%d' % bu], writes=[('hT', fc, tc2)])
            for nh in range(2):
                nsl = slice(nh * 512, (nh + 1) * 512)
                for t8 in range(8):
                    tt = half * 8 + t8
                    tsl = slice(tt * 128, (tt + 1) * 128)
                    pb = 4 + (oi % 2)
                    ob = oi % 2
                    oi += 1
                    for fc in range(NFC):
                        P.pe(lambda e, fc=fc, t8=t8, pb=pb, nh=nh: e.matmul(ps[pb], lhsT=hT[:, fc, t8 * 128:(t8 + 1) * 128], rhs=Wd[nh][:, fc, :], start=(fc == 0), stop=(fc == NFC - 1)),
                             reads=[('hT', fc, t8 // 4), ('Wd', nh)], writes=['ps%d' % pb])
                    P.dve(lambda e, ob=ob, pb=pb, tt=tt, nsl=nsl: e.tensor_tensor(out=obuf[ob], in0=ps[pb], in1=x1[:, tt, nsl], op=ALU.add),
                          reads=['ps%d' % pb, ('x1', tt)], writes=[('sgl', ob)])
                    P.dma('sp', lambda e, ob=ob, tsl=tsl, nsl=nsl: e.dma_start(out=y[tsl, nsl], in_=obuf[ob]), reads=[('sgl', ob)], writes=[('y', ob)], sem=('yo', ob))

    if 'stop_s5' not in dbg:
        P.barrier()
        phase_merge()
        dump("mergedT", mergedT, [128, 8, L])
        P.barrier()
        phase_out()
        dump("x1", x1, [128, NT, D])
        P.barrier()
        phase_ffn()

    outs = [('y', 0), ('y', 1)] + ['dbg_' + n for n in dbg_out]
    P.final_wait(outs)
    P.emit()
    return nc, dbg_out


_NC_CACHE = {}


def kernel(**inputs):
    if 'nc' not in _NC_CACHE:
        _NC_CACHE['nc'] = build_nc()[0]
    nc = _NC_CACHE['nc']
    x = np.ascontiguousarray(np.asarray(inputs['x'], dtype=np.float32))
    shared = {}
    for k, v in inputs.items():
        if k == 'x':
            continue
        a = np.asarray(v, dtype=np.float32)
        a = a.reshape(a.shape[1:])
        if k in ('norm1_g', 'norm2_g', 'q_norm_g', 'k_norm_g', 'idx_k_norm_g'):
            a = a.reshape(1, -1)
        shared[k] = np.ascontiguousarray(a)
    in_maps = []
    for c in range(8):
        m = dict(shared)
        m['x'] = x[c]
        in_maps.append(m)
    res = run_bass_kernel_spmd(nc, in_maps, core_ids=list(range(8)))
    return np.stack([r['y'] for r in res.results], axis=0)
```

```python
import math
import numpy as np
import concourse.bass as bass
import concourse.mybir as mybir
from concourse.bass_utils import run_bass_kernel_spmd

F32 = mybir.dt.float32
BF16 = mybir.dt.bfloat16
I32 = mybir.dt.int32
AF = mybir.ActivationFunctionType
ALU = mybir.AluOpType
AX = mybir.AxisListType

ENGS = ("pe", "act", "dve", "pool", "sp")

L = 2048
D = 1024
NT = L // 128
DFF = 2816
NFC = DFF // 128
EPS = 1e-6
TM = 16
NM = L // TM
NEG = -30000.0


class Prog:
    def __init__(self, nc):
        self.nc = nc
        self.ops = []
        self.seq = {e: 0 for e in ENGS}
        self.lastw = {}
        self.readers = {}
        self.seen = {e: {} for e in ENGS}
        self.dma_lists = {}
        self.dma_kind = {}
        self.needed = set()
        self.last_real = {e: 0 for e in ENGS}

    def defer(self):
        self.deferq = []
        self.deferring = True

    def pause_defer(self):
        self.deferring = False

    def flush(self, n=None):
        q = getattr(self, 'deferq', None) or []
        n = len(q) if n is None else min(n, len(q))
        was = getattr(self, 'deferring', False)
        self.deferring = False
        for _ in range(n):
            args = q.pop(0)
            self._add(*args)
        self.deferring = was

    def _add(self, eng, fn, reads, writes, dma=None, dma_kind="slot"):
        if getattr(self, 'deferring', False):
            self.deferq.append((eng, fn, list(reads), list(writes), dma, dma_kind))
            return None
        self.seq[eng] += 1
        myseq = self.seq[eng]
        deps = {}

        def need(tok):
            if tok is None:
                return
            if tok[0] == 'e':
                _, pe, ps = tok
                if pe == 'pe' and eng == 'pe':
                    return
                k = ('e', pe)
                deps[k] = max(deps.get(k, 0), ps)
            else:
                _, sk, pos = tok
                if dma is not None and sk == dma and dma_kind == 'group':
                    return
                k = ('d', sk)
                deps[k] = max(deps.get(k, 0), pos)

        for k in reads:
            need(self.lastw.get(k))
        for k in writes:
            need(self.lastw.get(k))
            for r in self.readers.get(k, ()):
                need(r)
        waits = []
        seen = self.seen[eng]
        for k, v in deps.items():
            if seen.get(k, 0) >= v:
                continue
            seen[k] = v
            waits.append((k, v))
            if k[0] == 'e':
                self.needed.add((k[1], v))
        if dma is not None:
            lst = self.dma_lists.setdefault(dma, [])
            self.dma_kind[dma] = dma_kind
            lst.append(len(self.ops))
            tok = ('d', dma, len(lst))
        else:
            tok = ('e', eng, myseq)
            if fn is not None:
                self.last_real[eng] = myseq
        self.ops.append(dict(eng=eng, fn=fn, waits=waits, dma=dma, seq=myseq))
        for k in reads:
            self.readers.setdefault(k, []).append(tok)
        for k in writes:
            self.lastw[k] = tok
            self.readers[k] = []
        return tok

    def pe(self, fn, reads=(), writes=()):
        return self._add('pe', fn, reads, writes)

    def act(self, fn, reads=(), writes=()):
        return self._add('act', fn, reads, writes)

    def dve(self, fn, reads=(), writes=()):
        return self._add('dve', fn, reads, writes)

    def pool(self, fn, reads=(), writes=()):
        return self._add('pool', fn, reads, writes)

    def on(self, eng, fn, reads=(), writes=()):
        return self._add(eng, fn, reads, writes)

    def dma(self, eng, fn, reads=(), writes=(), sem="misc", kind="slot"):
        return self._add(eng, fn, reads, writes, dma=sem, dma_kind=kind)

    def barrier(self):
        snap_e = dict(self.last_real)
        snap_d = {k: len(v) for k, v in self.dma_lists.items()}
        for e in ENGS:
            waits = []
            seen = self.seen[e]
            for e2 in ENGS:
                if e2 == 'sp' or snap_e[e2] == 0:
                    continue
                if e2 == e and e == 'pe':
                    continue
                k = ('e', e2)
                if seen.get(k, 0) >= snap_e[e2]:
                    continue
                seen[k] = snap_e[e2]
                waits.append((k, snap_e[e2]))
                self.needed.add((e2, snap_e[e2]))
            for sk, pos in snap_d.items():
                k = ('d', sk)
                if seen.get(k, 0) >= pos:
                    continue
                seen[k] = pos
                waits.append((k, pos))
            self.ops.append(dict(eng=e, fn=None, waits=waits, dma=None, seq=-1))

    def final_wait(self, keys):
        self._add('sp', None, reads=keys, writes=())

    def emit(self):
        nc = self.nc
        semval = {}
        cnt = {e: 0 for e in ENGS}
        for op in self.ops:
            if op['dma'] is None and op['fn'] is not None and (op['eng'], op['seq']) in self.needed:
                cnt[op['eng']] += 1
                semval[(op['eng'], op['seq'])] = cnt[op['eng']]
        esem = {e: nc.alloc_semaphore("s_" + e) for e in ENGS}
        dsem = {}
        for i, k in enumerate(self.dma_lists):
            dsem[k] = nc.alloc_semaphore("d_%d" % i)

        def run(ename, eng):
            for op in self.ops:
                if op['eng'] != ename:
                    continue
                for (k, v) in op['waits']:
                    if k[0] == 'e':
                        eng.wait_ge(esem[k[1]], semval[(k[1], v)])
                    else:
                        sk = k[1]
                        if self.dma_kind[sk] == 'group':
                            eng.wait_ge(dsem[sk], 16 * len(self.dma_lists[sk]))
                        else:
                            eng.wait_ge(dsem[sk], 16 * v)
                if op['fn'] is None:
                    continue
                inst = op['fn'](eng)
                if op['dma'] is not None:
                    inst.then_inc(dsem[op['dma']], 16)
                elif (ename, op['seq']) in semval:
                    inst.then_inc(esem[ename], 1)

        with nc.Block() as block:
            @block.tensor
            def _(eng):
                run('pe', eng)

            @block.scalar
            def _(eng):
                run('act', eng)

            @block.vector
            def _(eng):
                run('dve', eng)

            @block.gpsimd
            def _(eng):
                run('pool', eng)

            @block.sync
            def _(eng):
                run('sp', eng)


class Region:
    def __init__(self, arena, start, end):
        self.arena, self.start, self.end, self.cur = arena, start, end, start

    def take(self, shape, dt):
        n = int(np.prod(shape[1:]))
        esz = 4 if dt in (F32, I32) else 2
        nb = (n * esz + 63) // 64 * 64
        off = self.cur
        self.cur += nb
        assert self.cur <= self.end, ("region overflow", shape, self.cur, self.end)
        v = self.arena[:, off // 2: off // 2 + (n * esz) // 2]
        if dt != BF16:
            v = v.bitcast(dt)
        if len(shape) == 3:
            v = v.rearrange("p (a b) -> p a b", a=shape[1])
        elif len(shape) == 4:
            v = v.rearrange("p (a b c) -> p a b c", a=shape[1], b=shape[2])
        if shape[0] < 128:
            v = v[0:shape[0]]
        return v


def build_nc(dbg=()):
    nc = bass.Bass("TRN2", target_bir_lowering=False)
    P = Prog(nc)

    def din(name, shape):
        return nc.dram_tensor(name, list(shape), F32, kind="ExternalInput").ap()

    x = din("x", [L, D])
    norm1_g = din("norm1_g", [1, D])
    w_in = din("w_in", [D, 3524])
    A_re = din("A_re", [32, 64])
    A_im = din("A_im", [32, 64])
    log_dt = din("log_dt", [32])
    B_re = din("B_re", [32, 64, 16])
    B_im = din("B_im", [32, 64, 16])
    C_re = din("C_re", [32, 16, 64])
    C_im = din("C_im", [32, 16, 64])
    D_skip = din("D_skip", [512])
    w_glu = din("w_glu", [512, 512])
    b_glu = din("b_glu", [512])
    q_norm_g = din("q_norm_g", [1, 64])
    k_norm_g = din("k_norm_g", [1, 64])
    idx_k_norm_g = din("idx_k_norm_g", [1, 64])
    w_proj_ssm = din("w_proj_ssm", [512, D])
    w_proj_attn = din("w_proj_attn", [512, D])
    w_out = din("w_out", [D, D])
    norm2_g = din("norm2_g", [1, D])
    w_ffn_gate = din("w_ffn_gate", [D, DFF])
    w_ffn_up = din("w_ffn_up", [D, DFF])
    w_ffn_down = din("w_ffn_down", [DFF, D])
    y = nc.dram_tensor("y", [L, D], F32, kind="ExternalOutput").ap()
    dbg_out = {}

    ARENA_BYTES = 210944
    arena = nc.alloc_sbuf_tensor("arena", [128, ARENA_BYTES // 2], BF16)
    psall = nc.alloc_psum_tensor("psall", [128, 8 * 512], F32)
    ps = [psall[:, i * 512:(i + 1) * 512] for i in range(8)]

    def psb(i):
        return ps[i].bitcast(BF16).rearrange("p (a b) -> p a b", a=8)

    RC_ = Region(arena, 0, 4096)
    XN_ = Region(arena, 4096, 36864)
    RCm = Region(arena, 36864, 69632)
    RA_ = Region(arena, 69632, 86016)
    RB_ = Region(arena, 86016, 102400)
    RD_ = Region(arena, 102400, 130048)
    SCR0, SCR1 = 130048, ARENA_BYTES

    ident_f = RC_.take([128, 128], F32)
    ident_bf = RC_.take([128, 128], BF16)
    cbias = RC_.take([128, 128], BF16)
    cneg = RC_.take([128, 128], F32)
    ident4 = RC_.take([128, 4, 128], BF16)
    iota_i = RC_.take([128, 128], I32)
    zeros_bf = RC_.take([128, 128], BF16)

    P.pool(lambda e: e.iota(iota_i, pattern=[[1, 128]], base=0, channel_multiplier=-1), writes=['iota'])
    P.dve(lambda e: e.tensor_single_scalar(out=ident_f, in_=iota_i, scalar=0.0, op=ALU.is_equal), reads=['iota'], writes=['ident_f'])
    P.dve(lambda e: e.tensor_copy(out=ident_bf, in_=ident_f), reads=['ident_f'], writes=['ident_bf'])
    P.dve(lambda e: e.tensor_scalar(out=cneg, in0=iota_i, scalar1=0.0, scalar2=-1.0e30, op0=ALU.is_gt, op1=ALU.mult), reads=['iota'], writes=['cneg'])
    P.dve(lambda e: e.tensor_scalar(out=cbias, in0=iota_i, scalar1=0.0, scalar2=NEG, op0=ALU.is_gt, op1=ALU.mult), reads=['iota'], writes=['cbias'])
    for j in range(4):
        P.dve(lambda e, j=j: e.tensor_copy(out=ident4[:, j, :], in_=ident_f), reads=['ident_f'], writes=['ident4'])
    P.dve(lambda e: e.memset(zeros_bf, 0.0), writes=['zeros'])

    xnT = XN_.take([128, 8, L], BF16)
    uT = RA_.take([128, 4, L], BF16)
    ysT = uT
    ygT = RB_.take([128, 4, L], BF16)
    yaT = ygT
    qT = RCm.take([128, 8, L], BF16)
    mergedT = qT
    qiT = RD_.take([64, 4, L], BF16)
    kT = RD_.take([64, L], BF16)
    kiT = RD_.take([64, L], BF16)
    vaug = RD_.take([128, NT, 65], BF16)
    wi_s = RD_.take([128, NT, 4], F32)

    def dump(name, ap, shape):
        if name in dbg:
            o = nc.dram_tensor("dbg_" + name, list(shape), ap.dtype, kind="ExternalOutput").ap()
            dbg_out[name] = o
            P.dma('sp', lambda e: e.dma_start(out=o, in_=ap), reads=list(P.lastw.keys()), writes=['dbg_' + name], sem=('dbg', name))

    w_in_v = w_in.rearrange("(kc p) n -> p kc n", p=128)
    W_SCALE = (4 ** -0.5) * (64 ** -0.5)

    def phase_ab(hook=None):
        S = Region(arena, SCR0, SCR1)
        Wu = S.take([128, 8, 512], BF16)
        g1b = S.take([128, D], F32)
        xbuf = [S.take([128, D], F32) for _ in range(3)]
        xnb = [S.take([128, D], BF16) for _ in range(3)]
        junk = S.take([128, D], BF16)
        ss1 = S.take([128, NT], F32)
        rt1 = S.take([128, NT], F32)
        rs1 = S.take([128, NT], F32)
        P.dma('pool', lambda e: e.dma_start(out=Wu, in_=w_in_v[:, :, 0:512]), writes=['Wu'], sem='wu')
        P.dma('sp', lambda e: e.dma_start(out=g1b, in_=norm1_g.to_broadcast([128, D])), writes=['g1b'], sem='g1b')
        NXB = 3

        def st_a(tt):
            b = tt % NXB
            tsl = slice(tt * 128, (tt + 1) * 128)
            P.dma('sp', lambda e, b=b, tsl=tsl: e.dma_start(out=xbuf[b], in_=x[tsl, :]), writes=[('xbuf', b)], sem=('x', b))
            P.act(lambda e, b=b, tt=tt: e.activation(out=junk, in_=xbuf[b], func=AF.Square, accum_out=ss1[:, tt:tt + 1]),
                  reads=[('xbuf', b)], writes=['junk', ('ss1', tt)])
            P.act(lambda e, tt=tt: e.activation(out=rt1[:, tt:tt + 1], in_=ss1[:, tt:tt + 1], func=AF.Sqrt, scale=1.0 / D, bias=EPS),
                  reads=[('ss1', tt)], writes=[('rt1', tt)])
            P.dve(lambda e, tt=tt: e.reciprocal(out=rs1[:, tt:tt + 1], in_=rt1[:, tt:tt + 1]), reads=[('rt1', tt)], writes=[('rs1', tt)])
            P.dve(lambda e, b=b, tt=tt: e.scalar_tensor_tensor(out=xnb[b], in0=xbuf[b], scalar=rs1[:, tt:tt + 1], in1=g1b, op0=ALU.mult, op1=ALU.mult),
                  reads=[('xbuf', b), ('rs1', tt), 'g1b'], writes=[('xnb', b)])

        pending = {}

        def st_b(tt):
            b = tt % NXB
            tsl = slice(tt * 128, (tt + 1) * 128)
            pt = 6 + (tt % 2)
            for kc in range(8):
                P.pe(lambda e, b=b, kc=kc, pt=pt: e.transpose(out=psb(pt)[:, kc, :], in_=xnb[b][:, kc * 128:(kc + 1) * 128], identity=ident_bf),
                     reads=[('xnb', b), 'ident_bf'], writes=['ps%d' % pt])
            P.act(lambda e, tsl=tsl, pt=pt: e.activation(out=xnT[:, :, tsl], in_=psb(pt), func=AF.Copy), reads=['ps%d' % pt], writes=[('xnT', tt)])
            for fn_ in pending.pop(tt, []):
                fn_()
            if tt % 4 == 3:
                tc = tt // 4
                csl = slice(tc * 512, (tc + 1) * 512)

                def u_mm(ct, tc=tc, csl=csl):
                    pb = 2 + (ct % 2)
                    for kc in range(8):
                        P.pe(lambda e, kc=kc, ct=ct, pb=pb, csl=csl: e.matmul(ps[pb], lhsT=Wu[:, kc, ct * 128:(ct + 1) * 128], rhs=xnT[:, kc, csl],
                                                                             start=(kc == 0), stop=(kc == 7)),
                             reads=[('xnT', t4) for t4 in range(tc * 4, tc * 4 + 4)] + ['Wu'], writes=['ps%d' % pb])

                def u_cp(ct, csl=csl):
                    pb = 2 + (ct % 2)
                    P.dve(lambda e, ct=ct, pb=pb, csl=csl: e.tensor_copy(out=uT[:, ct, csl], in_=ps[pb]), reads=['ps%d' % pb], writes=[('uT', ct)])
                u_mm(0)
                u_mm(1)
                pending.setdefault(tt + 1, []).extend([lambda: u_cp(0), lambda: u_cp(1), lambda: u_mm(2), lambda: u_mm(3)])
                pending.setdefault(tt + 2, []).extend([lambda: u_cp(2), lambda: u_cp(3)])

        st_a(0)
        st_a(1)
        for tt in range(NT):
            if tt + 2 < NT:
                st_a(tt + 2)
            st_b(tt)
            if hook is not None:
                hook(tt)
        for tt_ in sorted(pending):
            for fn_ in pending[tt_]:
                fn_()

    def phase_s5():
        Sa = Region(arena, RCm.start, RCm.end)
        Sb = Region(arena, RD_.start, SCR1)
        TWO_PI = 2.0 * math.pi

        def t16(n=1):
            return Sa.take([128, 16], F32) if n == 1 else Sa.take([128, n, 16], F32)
        AR, AI, LDT, DTt, MAG, ANG, ANG2, KF, R1, MSK, SIN, COS = [t16() for _ in range(12)]
        KI = Sa.take([128, 16], I32)
        ABR, ABI, DEN, RDEN, NR, COR, COI, T1, T2 = [t16() for _ in range(9)]
        AMr = [t16(), t16()]
        AMi = [t16(), t16()]
        PWR, PWI, NPWI = t16(2 * TM), t16(2 * TM), t16(2 * TM)
        PT1, PT2 = t16(2 * TM), t16(2 * TM)
        WKR, WKI = t16(TM), t16(TM)
        NS = int(round(math.log2(NM)))
        MUR, MUI, NMUI = t16(NS), t16(NS), t16(NS)
        BNr = Sa.take([128, 16, 32], F32)
        BNi = Sa.take([128, 16, 32], F32)
        CNr = Sb.take([32, 16, 128], BF16)
        CNi = Sb.take([32, 16, 128], BF16)
        CTr = Sa.take([128, 16, 32], F32)
        CTi = Sa.take([128, 16, 32], F32)
        CTr_bf = Sa.take([128, 16, 32], BF16)
        NCTi_bf = Sa.take([128, 16, 32], BF16)
        Dcol = Sa.take([128, 4], F32)
        diagD = Sa.take([128, 4, 128], BF16)
        bglu = Sa.take([128, 4], F32)
        Wglu = Sb.take([128, 4, 512], BF16)

        P.dma('sp', lambda e: e.dma_start(out=AR, in_=A_re.rearrange("(st g2) n -> (g2 n) st", g2=2), allow_slow_non_contiguous=True), writes=['AR'], sem='s5c', kind='group')
        P.dma('sp', lambda e: e.dma_start(out=AI, in_=A_im.rearrange("(st g2) n -> (g2 n) st", g2=2), allow_slow_non_contiguous=True), writes=['AI'], sem='s5c', kind='group')
        for g2 in range(2):
            P.dma('sp', lambda e, g2=g2: e.dma_start(out=LDT[64 * g2:64 * g2 + 64, :], in_=bass.AP(log_dt.tensor, g2, [[0, 64], [2, 16]]), allow_slow_non_contiguous=True),
                  writes=['LDT'], sem='s5c', kind='group')
        P.dma('sp', lambda e: e.dma_start(out=Dcol, in_=D_skip.rearrange("(c p) -> p c", p=128), allow_slow_non_contiguous=True), writes=['Dcol'], sem='s5c', kind='group')
        P.dma('sp', lambda e: e.dma_start(out=bglu, in_=b_glu.rearrange("(c p) -> p c", p=128), allow_slow_non_contiguous=True), writes=['bglu'], sem='s5c', kind='group')
        P.dve(lambda e: e.memset(BNr, 0.0), writes=['BNr'])
        P.dve(lambda e: e.memset(BNi, 0.0), writes=['BNi'])
        P.pool(lambda e: e.memset(CNr, 0.0), writes=['CNr'])
        P.pool(lambda e: e.memset(CNi, 0.0), writes=['CNi'])
        for g2 in range(2):
            for nm_, BN, Bd in (('BNr', BNr, B_re), ('BNi', BNi, B_im)):
                P.dma('sp', lambda e, g2=g2, BN=BN, Bd=Bd: e.dma_start(out=BN[64 * g2:64 * g2 + 64, :, 16 * g2:16 * g2 + 16],
                                                                        in_=bass.AP(Bd.tensor, g2 * 1024, [[16, 64], [2048, 16], [1, 16]])),
                      reads=[], writes=[nm_], sem='s5b', kind='group')
            for nm_, CN, Cd in (('CNr', CNr, C_re), ('CNi', CNi, C_im)):
                P.dma('pool', lambda e, g2=g2, CN=CN, Cd=Cd: e.dma_start(out=CN[16 * g2:16 * g2 + 16, :, 64 * g2:64 * g2 + 64],
                                                                          in_=bass.AP(Cd.tensor, g2 * 1024, [[64, 16], [2048, 16], [1, 64]])),
                      reads=[], writes=[nm_], sem='s5cn', kind='group')
        P.dma('pool', lambda e: e.dma_start(out=Wglu, in_=w_glu.rearrange("(kc p) n -> p kc n", p=128)), writes=['Wglu'], sem='wglu')

        def tt_(eng, out, a, b_, op, rk, wk):
            P.on(eng, lambda e: e.tensor_tensor(out=out, in0=a, in1=b_, op=op), reads=rk, writes=wk)

        def ts_(eng, out, a, s1, s2, op0, op1, rk, wk):
            if op1 is None:
                P.on(eng, lambda e: e.tensor_scalar(out=out, in0=a, scalar1=s1, scalar2=None, op0=op0), reads=rk, writes=wk)
            else:
                P.on(eng, lambda e: e.tensor_scalar(out=out, in0=a, scalar1=s1, scalar2=s2, op0=op0, op1=op1), reads=rk, writes=wk)

        P.act(lambda e: e.activation(out=DTt, in_=LDT, func=AF.Exp), reads=['LDT'], writes=['DT'])
        tt_('dve', MAG, AR, DTt, ALU.mult, ['AR', 'DT'], ['MAG'])
        P.act(lambda e: e.activation(out=MAG, in_=MAG, func=AF.Exp), reads=['MAG'], writes=['MAG'])
        tt_('dve', ANG, AI, DTt, ALU.mult, ['AI', 'DT'], ['ANG'])
        ts_('dve', ANG2, ANG, math.pi / 2, None, ALU.add, None, ['ANG'], ['ANG2'])

        def red_sin(out, ang, ko, ka):
            ts_('dve', KF, ang, 1.0 / TWO_PI, 0.5, ALU.mult, ALU.add, [ka], ['KF'])
            P.dve(lambda e: e.tensor_copy(out=KI, in_=KF), reads=['KF'], writes=['KI'])
            P.dve(lambda e: e.tensor_copy(out=KF, in_=KI), reads=['KI'], writes=['KF'])
            P.dve(lambda e: e.scalar_tensor_tensor(out=R1, in0=KF, scalar=-TWO_PI, in1=ang, op0=ALU.mult, op1=ALU.add), reads=['KF', ka], writes=['R1'])
            ts_('dve', MSK, R1, -math.pi, None, ALU.is_lt, None, ['R1'], ['MSK'])
            P.dve(lambda e: e.scalar_tensor_tensor(out=R1, in0=MSK, scalar=TWO_PI, in1=R1, op0=ALU.mult, op1=ALU.add), reads=['MSK', 'R1'], writes=['R1'])
            ts_('dve', MSK, R1, math.pi, None, ALU.is_gt, None, ['R1'], ['MSK'])
            P.dve(lambda e: e.scalar_tensor_tensor(out=R1, in0=MSK, scalar=-TWO_PI, in1=R1, op0=ALU.mult, op1=ALU.add), reads=['MSK', 'R1'], writes=['R1'])
            P.act(lambda e: e.activation(out=out, in_=R1, func=AF.Sin), reads=['R1'], writes=[ko])
        red_sin(SIN, ANG, 'SIN', 'ANG')
        red_sin(COS, ANG2, 'COS', 'ANG2')
        tt_('dve', ABR, MAG, COS, ALU.mult, ['MAG', 'COS'], ['ABR'])
        tt_('dve', ABI, MAG, SIN, ALU.mult, ['MAG', 'SIN'], ['ABI'])
        tt_('dve', DEN, AR, AR, ALU.mult, ['AR'], ['DEN'])
        tt_('dve', T1, AI, AI, ALU.mult, ['AI'], ['T1'])
        tt_('dve', DEN, DEN, T1, ALU.add, ['DEN', 'T1'], ['DEN'])
        P.dve(lambda e: e.reciprocal(out=RDEN, in_=DEN), reads=['DEN'], writes=['RDEN'])
        ts_('dve', NR, ABR, -1.0, None, ALU.add, None, ['ABR'], ['NR'])
        tt_('dve', T1, NR, AR, ALU.mult, ['NR', 'AR'], ['T1'])
        tt_('dve', T2, ABI, AI, ALU.mult, ['ABI', 'AI'], ['T2'])
        tt_('dve', T1, T1, T2, ALU.add, ['T1', 'T2'], ['T1'])
        tt_('dve', COR, T1, RDEN, ALU.mult, ['T1', 'RDEN'], ['COR'])
        tt_('dve', T1, ABI, AR, ALU.mult, ['ABI', 'AR'], ['T1'])
        tt_('dve', T2, NR, AI, ALU.mult, ['NR', 'AI'], ['T2'])
        tt_('dve', T1, T1, T2, ALU.subtract, ['T1', 'T2'], ['T1'])
        tt_('dve', COI, T1, RDEN, ALU.mult, ['T1', 'RDEN'], ['COI'])

        P.dve(lambda e: e.memset(PWR[:, 0, :], 1.0), writes=['PWR'])
        P.dve(lambda e: e.memset(PWI[:, 0, :], 0.0), writes=['PWI'])
        P.dve(lambda e: e.tensor_copy(out=AMr[0], in_=ABR), reads=['ABR'], writes=[('AMr', 0)])
        P.dve(lambda e: e.tensor_copy(out=AMi[0], in_=ABI), reads=['ABI'], writes=[('AMi', 0)])
        m = 1
        cur = 0
        while m < 2 * TM:
            bc = lambda t, m=m: t.unsqueeze(1).to_broadcast([128, m, 16])
            ar_, ai_ = AMr[cur], AMi[cur]
            kr, ki_ = ('AMr', cur), ('AMi', cur)
            tt_('dve', PT1[:, 0:m, :], PWR[:, 0:m, :], bc(ar_), ALU.mult, ['PWR', kr], ['PT1'])
            tt_('dve', PT2[:, 0:m, :], PWI[:, 0:m, :], bc(ai_), ALU.mult, ['PWI', ki_], ['PT2'])
            tt_('dve', PWR[:, m:2 * m, :], PT1[:, 0:m, :], PT2[:, 0:m, :], ALU.subtract, ['PT1', 'PT2', 'PWR'], ['PWR'])
            tt_('dve', PT1[:, 0:m, :], PWR[:, 0:m, :], bc(ai_), ALU.mult, ['PWR', ki_], ['PT1'])
            tt_('dve', PT2[:, 0:m, :], PWI[:, 0:m, :], bc(ar_), ALU.mult, ['PWI', kr], ['PT2'])
            tt_('dve', PWI[:, m:2 * m, :], PT1[:, 0:m, :], PT2[:, 0:m, :], ALU.add, ['PT1', 'PT2', 'PWI'], ['PWI'])
            nxt = 1 - cur
            tt_('dve', T1, ar_, ar_, ALU.mult, [kr], ['T1'])
            tt_('dve', T2, ai_, ai_, ALU.mult, [ki_], ['T2'])
            tt_('dve', AMr[nxt], T1, T2, ALU.subtract, ['T1', 'T2'], [('AMr', nxt)])
            tt_('dve', T1, ar_, ai_, ALU.mult, [kr, ki_], ['T1'])
            ts_('dve', AMi[nxt], T1, 2.0, None, ALU.mult, None, ['T1'], [('AMi', nxt)])
            cur = nxt
            m *= 2
        ts_('dve', NPWI, PWI, -1.0, None, ALU.mult, None, ['PWI'], ['NPWI'])
        bcc = lambda t: t.unsqueeze(1).to_broadcast([128, TM, 16])
        tt_('dve', PT1[:, 0:TM, :], PWR[:, 0:TM, :], bcc(COR), ALU.mult, ['PWR', 'COR'], ['PT1'])
        tt_('dve', PT2[:, 0:TM, :], PWI[:, 0:TM, :], bcc(COI), ALU.mult, ['PWI', 'COI'], ['PT2'])
        tt_('dve', WKR, PT1[:, 0:TM, :], PT2[:, 0:TM, :], ALU.subtract, ['PT1', 'PT2'], ['WKR'])
        tt_('dve', PT1[:, 0:TM, :], PWR[:, 0:TM, :], bcc(COI), ALU.mult, ['PWR', 'COI'], ['PT1'])
        tt_('dve', PT2[:, 0:TM, :], PWI[:, 0:TM, :], bcc(COR), ALU.mult, ['PWI', 'COR'], ['PT2'])
        tt_('dve', WKI, PT1[:, 0:TM, :], PT2[:, 0:TM, :], ALU.add, ['PT1', 'PT2'], ['WKI'])
        P.dve(lambda e: e.tensor_copy(out=MUR[:, 0, :], in_=PWR[:, TM, :]), reads=['PWR'], writes=['MUR'])
        P.dve(lambda e: e.tensor_copy(out=MUI[:, 0, :], in_=PWI[:, TM, :]), reads=['PWI'], writes=['MUI'])
        for s_ in range(NS - 1):
            tt_('dve', T1, MUR[:, s_, :], MUR[:, s_, :], ALU.mult, ['MUR'], ['T1'])
            tt_('dve', T2, MUI[:, s_, :], MUI[:, s_, :], ALU.mult, ['MUI'], ['T2'])
            tt_('dve', MUR[:, s_ + 1, :], T1, T2, ALU.subtract, ['T1', 'T2', 'MUR'], ['MUR'])
            tt_('dve', T1, MUR[:, s_, :], MUI[:, s_, :], ALU.mult, ['MUR', 'MUI'], ['T1'])
            ts_('dve', MUI[:, s_ + 1, :], T1, 2.0, None, ALU.mult, None, ['T1', 'MUI'], ['MUI'])

        ts_('dve', NMUI, MUI, -1.0, None, ALU.mult, None, ['MUI'], ['NMUI'])
        for c_ in range(4):
            P.dve(lambda e, c_=c_: e.tensor_scalar(out=diagD[:, c_, :], in0=ident_f, scalar1=Dcol[:, c_:c_ + 1], scalar2=None, op0=ALU.mult), reads=['ident_f', 'Dcol'], writes=['diagD'])
        for st in range(16):
            P.pe(lambda e, st=st: e.matmul(ps[0][:, st * 32:(st + 1) * 32], lhsT=CNr[0:32, st, :], rhs=ident_bf[0:32, 0:32], start=True, stop=True),
                 reads=['CNr', 'ident_bf'], writes=['ps0'])
            P.pe(lambda e, st=st: e.matmul(ps[1][:, st * 32:(st + 1) * 32], lhsT=CNi[0:32, st, :], rhs=ident_bf[0:32, 0:32], start=True, stop=True),
                 reads=['CNi', 'ident_bf'], writes=['ps1'])
        p6v = ps[0].rearrange("p (s c) -> p s c", s=16)
        p7v = ps[1].rearrange("p (s c) -> p s c", s=16)
        P.act(lambda e: e.activation(out=CTr, in_=p6v, func=AF.Copy), reads=['ps0'], writes=['CTr'])
        P.act(lambda e: e.activation(out=CTi, in_=p7v, func=AF.Copy), reads=['ps1'], writes=['CTi'])
        P.act(lambda e: e.activation(out=CTr_bf, in_=p6v, func=AF.Copy), reads=['ps0'], writes=['CTr_bf'])
        P.act(lambda e: e.activation(out=NCTi_bf, in_=p7v, func=AF.Copy, scale=-1.0), reads=['ps1'], writes=['NCTi_bf'])

        BtR = [Sb.take([128, TM, 32], BF16) for _ in range(8)]
        BtI = [Sb.take([128, TM, 32], BF16) for _ in range(8)]
        CMr = [Sb.take([128, TM, 32], BF16) for _ in range(8)]
        CMi = [Sb.take([128, TM, 32], BF16) for _ in range(8)]
        yield
        P.barrier()
        Sal = Region(arena, RD_.start, RD_.start + 8192)
        Ta = [Sal.take([128, TM, 32], F32) for _ in range(2)]
        Tb = [Sal.take([128, TM, 32], F32) for _ in range(2)]
        WBr2 = [Sb.take([128, TM, 128], BF16) for _ in range(2)]
        WBi2 = [Sb.take([128, TM, 128], BF16) for _ in range(2)]
        KT2 = [Sb.take([128, TM, 128], BF16) for _ in range(2)]
        SA = [Sb.take([128, 4, 2, NM + 1], F32) for _ in range(4)]
        Q1, Q3 = [Sb.take([128, 4, NM], F32) for _ in range(2)]
        Sbf2 = [Sb.take([128, 4, 2, NM + 1], BF16) for _ in range(2)]
        ysf = Sb.take([128, L], F32)
        G1 = [Sb.take([128, 512], F32) for _ in range(1)] * 2
        G2 = [Sb.take([128, 512], F32) for _ in range(1)] * 2
        uTv = uT.rearrange("p c (m j) -> p c j m", j=TM)

        def sa_keys(b_):
            return [('SA', b_, p_, x_) for p_ in 'ri' for x_ in (0, 1, 2, 3, 'h')]
        zrhs = ident4.rearrange("p a b -> p (a b)")
        for i_, t_ in enumerate(SA):
            P.dve(lambda e, t_=t_: e.memset(t_, 0.0), writes=sa_keys(i_))
        psW = [psall[:, 0:2048], psall[:, 2048:4096]]
        psS = psall[:, 2048:4096]
        psY = psall[:, 0:2048]
        def s5_prep(ct):
            bs_ = 4 * (ct % 2)
            for sp_ in range(4):
                st = 4 * ct + sp_
                e1, e2 = 'dve', 'pool'
                bB = lambda t, st=st: t[:, st, :].unsqueeze(1).to_broadcast([128, TM, 32])
                bW = lambda t, st=st: t[:, 0:TM, st].unsqueeze(2).to_broadcast([128, TM, 32])
                bP = lambda t, st=st: t[:, 1:TM + 1, st].unsqueeze(2).to_broadcast([128, TM, 32])
                tt_(e1, Ta[0], bB(BNr), bW(WKR), ALU.mult, ['BNr', 'WKR'], [('Ta', 0)])
                tt_(e1, Tb[0], bB(BNi), bW(WKI), ALU.mult, ['BNi', 'WKI'], [('Tb', 0)])
                tt_(e1, BtR[bs_ + sp_], Ta[0], Tb[0], ALU.subtract, [('Ta', 0), ('Tb', 0)], [('BtR', ct % 2, sp_)])
                tt_(e2, Ta[1], bB(BNr), bW(WKI), ALU.mult, ['BNr', 'WKI'], [('Ta', 1)])
                tt_(e2, Tb[1], bB(BNi), bW(WKR), ALU.mult, ['BNi', 'WKR'], [('Tb', 1)])
                tt_(e2, BtI[bs_ + sp_], Ta[1], Tb[1], ALU.add, [('Ta', 1), ('Tb', 1)], [('BtI', ct % 2, sp_)])
                tt_(e1, Ta[0], bB(CTr), bP(PWR), ALU.mult, ['CTr', 'PWR'], [('Ta', 0)])
                tt_(e1, Tb[0], bB(CTi), bP(PWI), ALU.mult, ['CTi', 'PWI'], [('Tb', 0)])
                tt_(e1, CMr[bs_ + sp_], Ta[0], Tb[0], ALU.subtract, [('Ta', 0), ('Tb', 0)], [('CMr', ct % 2, sp_)])
                e3 = 'dve'
                ti_ = 0 if e3 == 'dve' else 1
                tt_(e3, Ta[ti_], bB(CTr), bP(NPWI), ALU.mult, ['CTr', 'NPWI'], [('Ta', ti_)])
                tt_(e3, Tb[ti_], bB(CTi), bP(PWR), ALU.mult, ['CTi', 'PWR'], [('Tb', ti_)])
                tt_(e3, CMi[bs_ + sp_], Ta[ti_], Tb[ti_], ALU.subtract, [('Ta', ti_), ('Tb', ti_)], [('CMi', ct % 2, sp_)])
        def s5_main_a(ct):
            bs_ = 4 * (ct % 2)
            cb_ = ct % 2
            WBr, WBi, KT, Sbf = WBr2[cb_], WBi2[cb_], KT2[cb_], Sbf2[cb_]
            sb_ = 2 * cb_
            for ri, (Bt, WB, wk) in enumerate(((BtR, WBr, ('WBr', cb_)), (BtI, WBi, ('WBi', cb_)))):
                bkeys = ['ps%d' % (4 * ri + q) for q in range(4)]
                for i in range(TM):
                    k = TM - 1 - i
                    for sp_ in range(4):
                        P.pe(lambda e, ri=ri, i=i, k=k, sp_=sp_, Bt=Bt: e.matmul(psW[ri][32 * sp_:32 * sp_ + 32, i * 128:(i + 1) * 128], lhsT=Bt[bs_ + sp_][:, k, :], rhs=ident_bf,
                                                                              start=True, stop=True, tile_position=(0, 32 * sp_)),
                             reads=[('BtR' if ri == 0 else 'BtI', ct % 2, sp_), 'ident_bf'], writes=[bkeys[i // 4]])
                P.act(lambda e, ri=ri, WB=WB: e.activation(out=WB.rearrange("p a b -> p (a b)"), in_=psW[ri], func=AF.Copy), reads=bkeys, writes=[wk])
            for q in range(4):
                P.pe(lambda e, q=q: e.matmul(ps[q], lhsT=zeros_bf, rhs=zrhs, start=True, stop=False, skip_group_check=True),
                     reads=['zeros', 'ident4'], writes=['ps%d' % q])
            for tau in range(TM):
                for sp_ in range(4):
                    st = 4 * ct + sp_
                    o_ = psY[32 * sp_:32 * sp_ + 32, tau * 128 + 32 * sp_: tau * 128 + 32 * sp_ + 32]
                    P.pe(lambda e, o_=o_, tau=tau, sp_=sp_, st=st: e.matmul(o_, lhsT=BtR[bs_ + sp_][:, tau, :], rhs=CTr_bf[:, st, :], start=False, stop=False,
                                                                           tile_position=(0, 32 * sp_), skip_group_check=True),
                         reads=[('BtR', ct % 2, sp_), 'CTr_bf'], writes=['ps%d' % (tau // 4)])
                    P.pe(lambda e, o_=o_, tau=tau, sp_=sp_, st=st: e.matmul(o_, lhsT=BtI[bs_ + sp_][:, tau, :], rhs=NCTi_bf[:, st, :], start=False, stop=True,
                                                                           tile_position=(0, 32 * sp_), skip_group_check=True),
                         reads=[('BtI', ct % 2, sp_), 'NCTi_bf'], writes=['ps%d' % (tau // 4)])
            P.pe(lambda e, ct=ct: e.matmul(psY[:, 0:128], lhsT=diagD[:, ct, :], rhs=ident_bf, start=False, stop=True, skip_group_check=True),
                 reads=['diagD', 'ident_bf'], writes=['ps0'])
            P.act(lambda e: e.activation(out=KT.rearrange("p a b -> p (a b)"), in_=psY[:, 0:TM * 128], func=AF.Copy),
                  reads=['ps0', 'ps1', 'ps2', 'ps3'], writes=[('KT', cb_)])
            for ri, (WB, wk) in enumerate(((WBr, ('WBr', cb_)), (WBi, ('WBi', cb_)))):
                for i in range(TM):
                    for sp_ in range(4):
                        reg = sp_ * 512 + ri * NM
                        P.pe(lambda e, sp_=sp_, reg=reg, i=i, WB=WB, ct=ct: e.matmul(psS[:, reg:reg + NM], lhsT=WB[32 * sp_:32 * sp_ + 32, i, :], rhs=uTv[32 * sp_:32 * sp_ + 32, ct, i, :],
                                                                                    start=(i == 0), stop=(i == TM - 1), tile_position=(32 * sp_, 0)),
                             reads=[wk, ('uT', ct)], writes=['ps%d' % (4 + reg // 512)])
            ci_ = 0
            P.act(lambda e: e.activation(out=SA[sb_][:, :, :, 1:NM + 1],
                                         in_=psS.rearrange("p (a c) -> p a c", a=4)[:, :, 0:2 * NM].rearrange("p a (b c) -> p a b c", b=2), func=AF.Copy),
                  reads=['ps4', 'ps5', 'ps6', 'ps7'], writes=sa_keys(sb_))
        def s5_scan(ct):
            bs_ = 4 * (ct % 2)
            cb_ = ct % 2
            WBr, WBi, KT, Sbf = WBr2[cb_], WBi2[cb_], KT2[cb_], Sbf2[cb_]
            sb_ = 2 * cb_
            ci_ = 0
            for s_ in range(NS):
                d = 1 << s_
                n_ = NM - d
                cur, nxt = SA[sb_ + ci_], SA[sb_ + 1 - ci_]
                for sp_ in range(4):
                    st = 4 * ct + sp_
                    mr = MUR[:, s_, st:st + 1]
                    mi = MUI[:, s_, st:st + 1]
                    nmi = NMUI[:, s_, st:st + 1]
                    cr, ci = cur[:, sp_, 0, 1:NM + 1], cur[:, sp_, 1, 1:NM + 1]
                    nr_, ni_ = nxt[:, sp_, 0, 1:NM + 1], nxt[:, sp_, 1, 1:NM + 1]
                    q1, q3 = Q1[:, sp_, :], Q3[:, sp_, :]
                    kc_ = [('SA', sb_ + ci_, 'r', sp_), ('SA', sb_ + ci_, 'i', sp_), ('SA', sb_ + ci_, 'r', 'h'), ('SA', sb_ + ci_, 'i', 'h')]
                    P.dve(lambda e, q1=q1, cr=cr, mr=mr, n_=n_, d=d: e.scalar_tensor_tensor(out=q1[:, 0:n_], in0=cr[:, 0:n_], scalar=mr, in1=cr[:, d:NM], op0=ALU.mult, op1=ALU.add),
                          reads=kc_ + ['MUR'], writes=[('Q1', sp_)])
                    P.dve(lambda e, q1=q1, ci=ci, nmi=nmi, nr_=nr_, n_=n_, d=d: e.scalar_tensor_tensor(out=nr_[:, d:NM], in0=ci[:, 0:n_], scalar=nmi, in1=q1[:, 0:n_], op0=ALU.mult, op1=ALU.add),
                          reads=kc_ + ['NMUI', ('Q1', sp_)], writes=[('SA', sb_ + 1 - ci_, 'r', sp_)])
                    P.dve(lambda e, q3=q3, ci=ci, mr=mr, n_=n_, d=d: e.scalar_tensor_tensor(out=q3[:, 0:n_], in0=ci[:, 0:n_], scalar=mr, in1=ci[:, d:NM], op0=ALU.mult, op1=ALU.add),
                          reads=kc_ + ['MUR'], writes=[('Q3', sp_)])
                    P.dve(lambda e, q3=q3, cr=cr, mi=mi, ni_=ni_, n_=n_, d=d: e.scalar_tensor_tensor(out=ni_[:, d:NM], in0=cr[:, 0:n_], scalar=mi, in1=q3[:, 0:n_], op0=ALU.mult, op1=ALU.add),
                          reads=kc_ + ['MUI', ('Q3', sp_)], writes=[('SA', sb_ + 1 - ci_, 'i', sp_)])
                P.pool(lambda e, nxt=nxt, cur=cur, d=d: e.tensor_copy(out=nxt[:, :, :, 1:1 + d], in_=cur[:, :, :, 1:1 + d]), reads=sa_keys(sb_ + ci_), writes=[('SA', sb_ + 1 - ci_, 'r', 'h'), ('SA', sb_ + 1 - ci_, 'i', 'h')])
                ci_ = 1 - ci_
            kfin = sa_keys(sb_ + ci_)
            P.act(lambda e, fin=SA[sb_ + ci_]: e.activation(out=Sbf, in_=fin, func=AF.Copy), reads=kfin, writes=[('Sbf', cb_)])
        def s5_c(ct):
            bs_ = 4 * (ct % 2)
            cb_ = ct % 2
            WBr, WBi, KT, Sbf = WBr2[cb_], WBi2[cb_], KT2[cb_], Sbf2[cb_]
            sb_ = 2 * cb_
            for q in range(4):
                P.pe(lambda e, q=q: e.matmul(ps[q], lhsT=zeros_bf, rhs=zrhs, start=True, stop=False, skip_group_check=True),
                     reads=['zeros', 'ident4'], writes=['ps%d' % q])
            for j in range(TM):
                bk = 'ps%d' % (j * NM // 512)
                for i in range(j + 1):
                    P.pe(lambda e, j=j, i=i, ct=ct: e.matmul(psY[:, j * NM:(j + 1) * NM], lhsT=KT[:, j - i, :], rhs=uTv[:, ct, i, :], start=False, stop=False, skip_group_check=True),
                         reads=[('KT', cb_), ('uT', ct)], writes=[bk])
                for sp_ in range(4):
                    o_ = psY[32 * sp_:32 * sp_ + 32, j * NM:(j + 1) * NM]
                    P.pe(lambda e, o_=o_, j=j, sp_=sp_: e.matmul(o_, lhsT=CMr[bs_ + sp_][:, j, :], rhs=Sbf[:, sp_, 0, 0:NM], start=False, stop=False,
                                                                tile_position=(0, 32 * sp_), skip_group_check=True),
                         reads=[('CMr', ct % 2, sp_), ('Sbf', cb_)], writes=[bk])
                    P.pe(lambda e, o_=o_, j=j, sp_=sp_: e.matmul(o_, lhsT=CMi[bs_ + sp_][:, j, :], rhs=Sbf[:, sp_, 1, 0:NM], start=False, stop=True,
                                                                tile_position=(0, 32 * sp_), skip_group_check=True),
                         reads=[('CMi', ct % 2, sp_), ('Sbf', cb_)], writes=[bk])
            P.act(lambda e: e.activation(out=ysf.rearrange("p (m j) -> p j m", j=TM), in_=psY.rearrange("p (j m) -> p j m", j=TM), func=AF.Copy),
                  reads=['ps0', 'ps1', 'ps2', 'ps3'], writes=['ysf'])
            if ct == 0:
                dump("ysf0", ysf, [128, L])
        def s5_gelu(ct):
            bs_ = 4 * (ct % 2)
            cb_ = ct % 2
            WBr, WBi, KT, Sbf = WBr2[cb_], WBi2[cb_], KT2[cb_], Sbf2[cb_]
            sb_ = 2 * cb_
            for c4 in range(4):
                csl = slice(c4 * 512, (c4 + 1) * 512)
                gb = c4 % 2
                P.act(lambda e, csl=csl, gb=gb: e.activation(out=G1[gb], in_=ysf[:, csl], func=AF.Square), reads=['ysf'], writes=[('G1', 0)])
                ts_('dve', G1[gb], G1[gb], 0.044715, 1.0, ALU.mult, ALU.add, [('G1', 0)], [('G1', 0)])
                tt_('dve', G1[gb], G1[gb], ysf[:, csl], ALU.mult, [('G1', 0), 'ysf'], [('G1', 0)])
                P.act(lambda e, gb=gb: e.activation(out=G2[gb], in_=G1[gb], func=AF.Sigmoid, scale=2.0 * math.sqrt(2.0 / math.pi)), reads=[('G1', 0)], writes=[('G2', 0)])
                tt_('dve', ygT[:, ct, csl], G2[gb], ysf[:, csl], ALU.mult, [('G2', 0), 'ysf'], [('ygT', ct, c4)])
        s5_prep(0)
        s5_prep(1)
        s5_main_a(0)
        s5_main_a(1)
        s5_scan(0)
        s5_scan(1)
        s5_c(0)
        s5_prep(2)
        s5_gelu(0)
        s5_c(1)
        s5_prep(3)
        s5_gelu(1)
        s5_main_a(2)
        s5_main_a(3)
        s5_scan(2)
        s5_scan(3)
        s5_c(2)
        s5_gelu(2)
        s5_c(3)
        s5_gelu(3)
        dump("ygT", ygT, [128, 4, L])
        P.barrier()
        for oc in range(4):
            for c4 in range(4):
                csl = slice(c4 * 512, (c4 + 1) * 512)
                pb = 6 + (c4 % 2)
                for kc in range(4):
                    P.pe(lambda e, oc=oc, kc=kc, csl=csl, pb=pb: e.matmul(ps[pb], lhsT=Wglu[:, kc, oc * 128:(oc + 1) * 128], rhs=ygT[:, kc, csl], start=(kc == 0), stop=(kc == 3)),
                         reads=['Wglu'] + [('ygT', kc, c4)], writes=['ps%d' % pb])
                gb = c4 % 2
                P.act(lambda e, oc=oc, pb=pb, gb=gb: e.activation(out=G2[gb], in_=ps[pb], func=AF.Sigmoid, bias=bglu[:, oc:oc + 1]), reads=['ps%d' % pb, 'bglu'], writes=[('G2', 0)])
                tt_('dve', ysT[:, oc, csl], G2[gb], ygT[:, oc, csl], ALU.mult, [('G2', 0), ('ygT', oc, c4)], [('ysT', oc, c4)])
        dump("ysT", ysT, [128, 4, L])

    def phase_c():
        S = Region(arena, SCR0, SCR1)
        Wq = S.take([128, 8, 512], BF16)
        Wkv = S.take([128, 8, 452], BF16)
        gqk = S.take([128, 64], F32)
        gkb = S.take([128, 64], F32)
        gib = S.take([128, 64], F32)
        junk = S.take([128, 64], BF16)
        sq = S.take([128, 512], F32)
        ssq = S.take([128, NT, 10], F32)
        rtq = S.take([128, NT, 10], F32)
        rsq = S.take([128, NT, 10], F32)
        qh = [S.take([128, 8, 64], BF16) for _ in range(2)]
        kk = [S.take([128, 6, 64], BF16) for _ in range(2)]
        P.dma('pool', lambda e: e.dma_start(out=Wq, in_=w_in_v[:, :, 512:1024]), writes=['Wq'], sem='wq')
        P.dma('pool', lambda e: e.dma_start(out=Wkv, in_=w_in_v[:, :, 1024:1476]), writes=['Wkv'], sem='wkv')
        P.dma('sp', lambda e: e.dma_start(out=gqk, in_=q_norm_g.to_broadcast([128, 64])), writes=['gqk'], sem='c0', kind='group')
        P.dma('sp', lambda e: e.dma_start(out=gkb, in_=k_norm_g.to_broadcast([128, 64])), writes=['gkb'], sem='c0', kind='group')
        P.dma('sp', lambda e: e.dma_start(out=gib, in_=idx_k_norm_g.to_broadcast([128, 64])), writes=['gib'], sem='c0', kind='group')
        P.dve(lambda e: e.tensor_tensor(out=gqk, in0=gqk, in1=gkb, op=ALU.mult), reads=['gqk', 'gkb'], writes=['gqk'])
        P.pool(lambda e: e.memset(vaug[:, :, 64:65], 1.0), writes=['vaug1'])
        def c_a(tt):
            b = tt % 2
            tsl = slice(tt * 128, (tt + 1) * 128)
            pq, pk = 2 * b, 2 * b + 1
            kq, kk_ = 'ps%d' % pq, 'ps%d' % pk
            for kc in range(8):
                P.pe(lambda e, kc=kc, tsl=tsl, pq=pq: e.matmul(ps[pq], lhsT=xnT[:, kc, tsl], rhs=Wq[:, kc, :], start=(kc == 0), stop=(kc == 7)),
                     reads=[('xnT', tt), 'Wq'], writes=[kq])
            for kc in range(8):
                P.pe(lambda e, kc=kc, tsl=tsl, pk=pk: e.matmul(ps[pk][:, 0:452], lhsT=xnT[:, kc, tsl], rhs=Wkv[:, kc, :], start=(kc == 0), stop=(kc == 7)),
                     reads=[('xnT', tt), 'Wkv'], writes=[kk_])

        def c_a2(tt):
            b = tt % 2
            pq, pk = 2 * b, 2 * b + 1
            kq, kk_ = 'ps%d' % pq, 'ps%d' % pk
            P.act(lambda e, pq=pq: e.activation(out=sq, in_=ps[pq], func=AF.Square), reads=[kq], writes=['sq'])
            P.dve(lambda e, tt=tt: e.tensor_reduce(out=ssq[:, tt, 0:8], in_=sq.rearrange("p (h d) -> p h d", h=8), axis=AX.X, op=ALU.add),
                  reads=['sq'], writes=[('ssq', tt)])
            P.act(lambda e, tt=tt, pk=pk: e.activation(out=junk, in_=ps[pk][:, 0:64], func=AF.Square, accum_out=ssq[:, tt, 8:9]),
                  reads=[kk_], writes=['junk', ('ssq', tt)])
            P.act(lambda e, tt=tt, pk=pk: e.activation(out=junk, in_=ps[pk][:, 384:448], func=AF.Square, accum_out=ssq[:, tt, 9:10]),
                  reads=[kk_], writes=['junk', ('ssq', tt)])
            P.act(lambda e, tt=tt: e.activation(out=rtq[:, tt, :], in_=ssq[:, tt, :], func=AF.Sqrt, scale=1.0 / 64, bias=EPS),
                  reads=[('ssq', tt)], writes=[('rtq', tt)])
            P.dve(lambda e, tt=tt: e.reciprocal(out=rsq[:, tt, :], in_=rtq[:, tt, :]), reads=[('rtq', tt)], writes=[('rsq', tt)])
            P.dve(lambda e, tt=tt, b=b, pq=pq: e.tensor_tensor(out=qh[b], in0=ps[pq].rearrange("p (h d) -> p h d", h=8),
                                                              in1=rsq[:, tt, 0:8].unsqueeze(2).to_broadcast([128, 8, 64]), op=ALU.mult),
                  reads=[kq, ('rsq', tt)], writes=[('qh', b)])
            P.dve(lambda e, tt=tt, b=b, pk=pk: e.scalar_tensor_tensor(out=kk[b][:, 0, :], in0=ps[pk][:, 0:64], scalar=rsq[:, tt, 8:9], in1=gqk, op0=ALU.mult, op1=ALU.mult),
                  reads=[kk_, ('rsq', tt), 'gqk'], writes=[('kk', b)])
            P.dve(lambda e, tt=tt, b=b, pk=pk: e.scalar_tensor_tensor(out=kk[b][:, 1, :], in0=ps[pk][:, 384:448], scalar=rsq[:, tt, 9:10], in1=gib, op0=ALU.mult, op1=ALU.mult),
                  reads=[kk_, ('rsq', tt), 'gib'], writes=[('kk', b)])
            P.act(lambda e, b=b, pk=pk: e.activation(out=kk[b][:, 2:6, :], in_=ps[pk][:, 128:384].rearrange("p (h d) -> p h d", h=4), func=AF.Copy),
                  reads=[kk_], writes=[('kk', b)])
            P.act(lambda e, tt=tt, pk=pk: e.activation(out=vaug[:, tt, 0:64], in_=ps[pk][:, 64:128], func=AF.Copy), reads=[kk_], writes=[('vaug', tt)])
            P.act(lambda e, tt=tt, pk=pk: e.activation(out=wi_s[:, tt, :], in_=ps[pk][:, 448:452], func=AF.Copy, scale=W_SCALE), reads=[kk_], writes=[('wi', tt)])

        def c_b(tt):
            b = tt % 2
            tsl = slice(tt * 128, (tt + 1) * 128)
            p6, p5 = 4 + 2 * b, 5 + 2 * b
            for h in range(8):
                P.pe(lambda e, b=b, h=h, p6=p6: e.transpose(out=psb(p6)[0:64, h, :], in_=qh[b][:, h, :], identity=ident_bf),
                     reads=[('qh', b), 'ident_bf'], writes=['ps%d' % p6])
            P.act(lambda e, tsl=tsl, p6=p6: e.activation(out=qT[0:64, :, tsl], in_=psb(p6)[0:64], func=AF.Copy), reads=['ps%d' % p6], writes=[('qT', tt)])
            for j in range(6):
                P.pe(lambda e, b=b, j=j, p5=p5: e.transpose(out=psb(p5)[0:64, j, :], in_=kk[b][:, j, :], identity=ident_bf),
                     reads=[('kk', b), 'ident_bf'], writes=['ps%d' % p5])
            P.dve(lambda e, tsl=tsl, p5=p5: e.tensor_copy(out=kT[:, tsl], in_=psb(p5)[0:64, 0, :]), reads=['ps%d' % p5], writes=[('kT', tt)])
            P.dve(lambda e, tsl=tsl, p5=p5: e.tensor_copy(out=kiT[:, tsl], in_=psb(p5)[0:64, 1, :]), reads=['ps%d' % p5], writes=[('kiT', tt)])
            P.dve(lambda e, tsl=tsl, p5=p5: e.tensor_copy(out=qiT[:, :, tsl], in_=psb(p5)[0:64, 2:6, :]), reads=['ps%d' % p5], writes=[('qiT', tt)])

        c_a(0)
        c_a(1)
        c_a2(0)
        for tt in range(NT):
            if tt + 2 < NT:
                c_a(tt + 2)
            if tt + 1 < NT:
                c_a2(tt + 1)
            c_b(tt)

    def phase_attn():
        S = Region(arena, SCR0, SCR1)
        G = 2
        isc = [[S.take([128, L], F32) for _ in range(G)] for _ in range(2)]
        bias = [[S.take([128, L], BF16) for _ in range(G)] for _ in range(2)]
        cum = S.take([128, L], F32)
        gtb = S.take([128, L], BF16)
        rr = [S.take([128, 512], F32) for _ in range(4)]
        tmpi = S.take([128, 512], F32)
        PT = [S.take([128, 1024], BF16) for _ in range(2)]
        rden = S.take([128, 8], F32)
        lnd = S.take([128, 8], F32)
        oh = [S.take([128, 8, 64], BF16) for _ in range(2)]
        NB_ = 31
        BITS = S.take([128, NB_], I32)
        n0 = S.take([128, G], F32)
        sig_f = S.take([128, G], F32)
        Pk = S.take([128, G], I32)
        cand = S.take([128, G], I32)
        uu = S.take([128, G], I32)
        C31 = S.take([128, 2], I32)
        cnt = S.take([128, G], F32)
        dd = S.take([128, G], F32)
        ngt = S.take([128, G], F32)
        rsel = S.take([128, G], F32)
        ps_s2 = [psall[:, 4 * 512:6 * 512], psall[:, 2 * 512:4 * 512]]
        zrhs = ident4.rearrange("p a b -> p (a b)")
        KSEL = 256
        for j in range(NB_):
            bb = 30 - j
            P.pool(lambda e, j=j, bb=bb: e.memset(BITS[:, j:j + 1], 1 << bb), writes=['BITS'])
        P.pool(lambda e: e.memset(C31, 31), writes=['C31'])

        def tile_of(k, g):
            return 2 + 2 * k + g
        GORDER = [1, 6, 5, 4, 3, 2, 0]
        pos_of = {k_: i_ for i_, k_ in enumerate(GORDER)}

        def idx(qt):
            k, g = (qt - 2) // 2, (qt - 2) % 2
            gp = pos_of[k] % 2
            qsl = slice(qt * 128, (qt + 1) * 128)
            Sv = 128 * (qt + 1)
            iscb = isc[gp][g]
            kI = ('isc', gp, g)
            for c in range((Sv + 511) // 512):
                c0, c1 = c * 512, min(Sv, (c + 1) * 512)
                w = c1 - c0
                for h in range(4):
                    P.pe(lambda e, h=h, c0=c0, c1=c1, w=w, qsl=qsl: e.matmul(ps[h][:, 0:w], lhsT=qiT[:, h, qsl], rhs=kiT[:, c0:c1], start=True, stop=True),
                         reads=[('qiT', qt)] + [('kiT', t) for t in range(c0 // 128, c1 // 128)], writes=['ps%d' % h])
                    P.act(lambda e, h=h, w=w: e.activation(out=rr[h][:, 0:w], in_=ps[h][:, 0:w], func=AF.Relu), reads=['ps%d' % h], writes=[('rr', h)])
                    if h == 0:
                        P.pool(lambda e, c0=c0, c1=c1, w=w, qt=qt, iscb=iscb: e.tensor_scalar(out=iscb[:, c0:c1], in0=rr[0][:, 0:w], scalar1=wi_s[:, qt, 0:1], scalar2=0.0, op0=ALU.mult, op1=ALU.add),
                               reads=[('rr', 0), ('wi', qt)], writes=[kI])
                    else:
                        P.pool(lambda e, h=h, w=w, qt=qt: e.tensor_scalar(out=tmpi[:, 0:w], in0=rr[h][:, 0:w], scalar1=wi_s[:, qt, h:h + 1], scalar2=0.0, op0=ALU.mult, op1=ALU.add),
                               reads=[('rr', h), ('wi', qt)], writes=['tmpi'])
                        P.pool(lambda e, c0=c0, c1=c1, w=w, iscb=iscb: e.tensor_tensor(out=iscb[:, c0:c1], in0=iscb[:, c0:c1], in1=tmpi[:, 0:w], op=ALU.add),
                               reads=['tmpi', kI], writes=[kI])
            P.pool(lambda e, qsl=qsl, iscb=iscb: e.tensor_tensor(out=iscb[:, qsl], in0=iscb[:, qsl], in1=cneg, op=ALU.add), reads=[kI, 'cneg'], writes=[kI])

        def topk_group(k):
            gp = pos_of[k] % 2
            tiles = [tile_of(k, g) for g in range(G)]
            Svs = [128 * (qt + 1) for qt in tiles]
            kIs = [('isc', gp, g) for g in range(G)]
            for g in range(G):
                P.dve(lambda e, g=g: e.tensor_scalar(out=bias[gp][g][:, 0:Svs[g]], in0=isc[gp][g][:, 0:Svs[g]], scalar1=0.0, scalar2=None, op0=ALU.is_ge, op1=ALU.add, accum_out=n0[:, g:g + 1]),
                      reads=[kIs[g]], writes=[('n0', g), ('bias', gp, g)])
            P.dve(lambda e: e.tensor_single_scalar(out=sig_f, in_=n0, scalar=float(KSEL), op=ALU.is_lt), reads=[('n0', g) for g in range(G)], writes=['sig_f'])
            P.dve(lambda e: e.tensor_single_scalar(out=dd, in_=sig_f, scalar=-1.0, op=ALU.mult), reads=['sig_f'], writes=['dd'])
            P.dve(lambda e: e.tensor_copy(out=Pk, in_=dd), reads=['dd'], writes=['Pk'])
            P.dve(lambda e: e.tensor_scalar(out=cand, in0=Pk, scalar1=BITS[:, 0:1], scalar2=None, op0=ALU.bitwise_xor), reads=['Pk', 'BITS'], writes=['cand'])
            for j in range(NB_):
                for g in range(G):
                    P.dve(lambda e, g=g: e.tensor_scalar(out=bias[gp][g][:, 0:Svs[g]], in0=isc[gp][g][:, 0:Svs[g]], scalar1=cand[:, g:g + 1].bitcast(F32), scalar2=0.5 - KSEL,
                                                         op0=ALU.is_ge, op1=ALU.add, accum_out=cnt[:, g:g + 1]),
                          reads=[kIs[g], 'cand'], writes=[('cnt', g), ('bias', gp, g)])
                P.dve(lambda e, j=j: e.tensor_scalar(out=uu, in0=cnt.bitcast(I32), scalar1=C31[:, 0:1], scalar2=BITS[:, j:j + 1], op0=ALU.arith_shift_right, op1=ALU.bitwise_and),
                      reads=[('cnt', g) for g in range(G)] + ['BITS', 'C31'], writes=['uu'])
                if j + 1 < NB_:
                    P.dve(lambda e, j=j: e.scalar_tensor_tensor(out=cand, in0=uu, scalar=BITS[:, j + 1:j + 2], in1=cand, op0=ALU.bitwise_xor, op1=ALU.bitwise_xor),
                          reads=['uu', 'BITS', 'cand'], writes=['cand'])
                else:
                    P.dve(lambda e: e.tensor_tensor(out=Pk, in0=uu, in1=cand, op=ALU.bitwise_xor), reads=['uu', 'cand'], writes=['Pk'])
            for g in range(G):
                Sv = Svs[g]
                iscb, biasb = isc[gp][g], bias[gp][g]
                kI, kB = kIs[g], ('bias', gp, g)
                thr = Pk[:, g:g + 1].bitcast(F32)
                P.dve(lambda e, Sv=Sv, iscb=iscb, thr=thr, g=g: e.tensor_scalar(out=gtb[:, 0:Sv], in0=iscb[:, 0:Sv], scalar1=thr, scalar2=None, op0=ALU.is_gt, op1=ALU.add, accum_out=ngt[:, g:g + 1]),
                      reads=[kI, 'Pk', 'gtb', 'cum'], writes=['gtb', 'ngt'])
                P.dve(lambda e, g=g: e.tensor_scalar(out=rsel[:, g:g + 1], in0=ngt[:, g:g + 1], scalar1=-1.0, scalar2=float(KSEL), op0=ALU.mult, op1=ALU.add), reads=['ngt'], writes=['rsel'])
                P.dve(lambda e, Sv=Sv, iscb=iscb, thr=thr: e.tensor_scalar(out=iscb[:, 0:Sv], in0=iscb[:, 0:Sv], scalar1=thr, scalar2=None, op0=ALU.is_equal),
                      reads=[kI, 'Pk'], writes=[kI])
                P.dve(lambda e, Sv=Sv, iscb=iscb: e.tensor_tensor_scan(out=cum[:, 0:Sv], data0=iscb[:, 0:Sv], data1=iscb[:, 0:Sv], initial=0.0, op0=ALU.add, op1=ALU.max),
                      reads=[kI], writes=['cum'])
                P.dve(lambda e, Sv=Sv, iscb=iscb, g=g: e.scalar_tensor_tensor(out=cum[:, 0:Sv], in0=cum[:, 0:Sv], scalar=rsel[:, g:g + 1], in1=iscb[:, 0:Sv], op0=ALU.is_le, op1=ALU.mult),
                      reads=['cum', 'rsel', kI], writes=['cum'])
                P.dve(lambda e, Sv=Sv: e.tensor_tensor(out=cum[:, 0:Sv], in0=cum[:, 0:Sv], in1=gtb[:, 0:Sv], op=ALU.add), reads=['cum', 'gtb'], writes=['cum'])
                P.dve(lambda e, Sv=Sv, biasb=biasb: e.tensor_scalar(out=biasb[:, 0:Sv], in0=cum[:, 0:Sv], scalar1=-NEG, scalar2=NEG, op0=ALU.mult, op1=ALU.add),
                      reads=['cum'], writes=[kB])

        def attn(qt):
            b = qt % 2
            qsl = slice(qt * 128, (qt + 1) * 128)
            if qt >= 2:
                k_, g_ = (qt - 2) // 2, (qt - 2) % 2
                biasb = bias[pos_of[k_] % 2][g_]
                kB = ('bias', pos_of[k_] % 2, g_)
            else:
                biasb, kB = None, None
            P.pe(lambda e: e.matmul(ps[6], lhsT=zeros_bf, rhs=zrhs, start=True, stop=False, skip_group_check=True), reads=['zeros', 'ident4'], writes=['ps6'])
            P.pe(lambda e: e.matmul(ps[7], lhsT=zeros_bf, rhs=zrhs, start=True, stop=False, skip_group_check=True), reads=['zeros', 'ident4'], writes=['ps7'])
            for kt in range(qt + 1):
                ksl = slice(kt * 128, (kt + 1) * 128)
                pb = kt % 2
                if qt >= 2:
                    bsrc, bkeys = biasb[:, ksl], [kB]
                elif kt == qt:
                    bsrc, bkeys = cbias, ['cbias']
                else:
                    bsrc, bkeys = None, []
                ps_s = ps_s2[kt % 2]
                sb0 = 4 if kt % 2 == 0 else 2
                for half in range(2):
                    P.pe(lambda e, half=half, ksl=ksl, qsl=qsl, ps_s=ps_s, last=(bsrc is None): e.matmul(ps_s[:, half * 512:(half + 1) * 512], lhsT=kT[:, ksl], rhs=qT[0:64, 4 * half:4 * half + 4, qsl],
                                                                                                       start=True, stop=last),
                         reads=[('kT', kt), ('qT', qt)], writes=['ps%d' % (sb0 + half)])
                    if bsrc is not None:
                        P.pe(lambda e, half=half, bsrc=bsrc, ps_s=ps_s: e.matmul(ps_s[:, half * 512:(half + 1) * 512], lhsT=bsrc, rhs=ident4, start=False, stop=True),
                             reads=bkeys + ['ident4'], writes=['ps%d' % (sb0 + half)])
                P.act(lambda e, pb=pb, ps_s=ps_s: e.activation(out=PT[pb], in_=ps_s, func=AF.Exp, scale=0.125), reads=['ps%d' % sb0, 'ps%d' % (sb0 + 1)], writes=[('PT', pb)])
                for h in range(8):
                    P.pe(lambda e, h=h, pb=pb, kt=kt, last=(kt == qt): e.matmul(ps[6 + h // 4][:, (h % 4) * 65:(h % 4) * 65 + 65], lhsT=PT[pb][:, h * 128:(h + 1) * 128], rhs=vaug[:, kt, :],
                                                                               start=False, stop=last, skip_group_check=True),
                         reads=[('PT', pb), ('vaug', kt), 'vaug1'], writes=['ps%d' % (6 + h // 4)])
            for hb in range(2):
                pv = ps[6 + hb][:, 0:260].rearrange("p (h d) -> p h d", h=4)
                P.act(lambda e, hb=hb, pv=pv: e.activation(out=lnd[:, 4 * hb:4 * hb + 4], in_=pv[:, :, 64], func=AF.Ln), reads=['ps%d' % (6 + hb)], writes=[('lnd', hb)])
                P.act(lambda e, hb=hb: e.activation(out=rden[:, 4 * hb:4 * hb + 4], in_=lnd[:, 4 * hb:4 * hb + 4], func=AF.Exp, scale=-1.0), reads=[('lnd', hb)], writes=[('rden', hb)])
            for h in range(8):
                P.act(lambda e, h=h, b=b: e.activation(out=oh[b][:, h, :], in_=ps[6 + h // 4][:, (h % 4) * 65:(h % 4) * 65 + 64], func=AF.Copy, scale=rden[:, h:h + 1]),
                      reads=['ps%d' % (6 + h // 4), ('rden', h // 4)], writes=[('oh', b)])
            for c in range(4):
                P.pe(lambda e, c=c, b=b: e.transpose(out=psb(0)[:, c, :], in_=oh[b].rearrange("p h d -> p (h d)")[:, c * 128:(c + 1) * 128], identity=ident_bf),
                     reads=[('oh', b), 'ident_bf'], writes=['ps0'])
            P.act(lambda e, qsl=qsl: e.activation(out=yaT[:, :, qsl], in_=psb(0)[:, 0:4, :], func=AF.Copy), reads=['ps0'], writes=[('yaT', qt)])

        NG = (NT - 2) // G
        assert sorted(GORDER) == list(range(NG))
        idx(tile_of(GORDER[0], 0))
        idx(tile_of(GORDER[0], 1))
        topk_group(GORDER[0])
        attn(0)
        attn(1)
        for p_ in range(NG):
            if p_ + 1 < NG:
                kn = GORDER[p_ + 1]
                idx(tile_of(kn, 0))
                idx(tile_of(kn, 1))
                topk_group(kn)
            attn(tile_of(GORDER[p_], 0))
            attn(tile_of(GORDER[p_], 1))

    s5gen = phase_s5()
    P.defer()
    next(s5gen)
    P.pause_defer()
    nq = len(P.deferq)
    per = (nq + NT - 3) // (NT - 2)
    phase_ab(hook=lambda tt: P.flush(per) if tt >= 1 else None)
    P.flush()
    dump("xnT", xnT, [128, 8, L])
    dump("uT", uT, [128, 4, L])
    P.barrier()
    for _ in s5gen:
        pass
    P.barrier()
    if 'stop_s5' not in dbg:
        phase_c()
        P.barrier()
        phase_attn()
        dump("yaT", yaT, [128, 4, L])

    def phase_merge():
        S = Region(arena, RD_.start, SCR1)
        Wps = S.take([128, 4, D], BF16)
        Wpa = S.take([128, 4, D], BF16)
        Wg1 = S.take([128, 8, D], BF16)
        Wg2 = S.take([128, 8, D], BF16)
        sg = [S.take([128, 512], F32) for _ in range(4)]
        mm = [S.take([128, 512], F32) for _ in range(4)]
        wpsv = w_proj_ssm.rearrange("(kc p) n -> p kc n", p=128)
        wpav = w_proj_attn.rearrange("(kc p) n -> p kc n", p=128)
        for oc in range(8):
            osl = slice(oc * 128, (oc + 1) * 128)
            P.dma('pool', lambda e, osl=osl: e.dma_start(out=Wps[:, :, osl], in_=wpsv[:, :, osl]), writes=[('Wps', oc)], sem=('wps', oc))
            P.dma('pool', lambda e, osl=osl, oc=oc: e.dma_start(out=Wg1[:, :, osl], in_=w_in_v[:, :, 1476 + oc * 128:1476 + (oc + 1) * 128]), writes=[('Wg1', oc)], sem=('wg1', oc))
            P.dma('pool', lambda e, osl=osl: e.dma_start(out=Wpa[:, :, osl], in_=wpav[:, :, osl]), writes=[('Wpa', oc)], sem=('wpa', oc))
            P.dma('pool', lambda e, osl=osl, oc=oc: e.dma_start(out=Wg2[:, :, osl], in_=w_in_v[:, :, 2500 + oc * 128:2500 + (oc + 1) * 128]), writes=[('Wg2', oc)], sem=('wg2', oc))
        it = 0
        for tc in range(4):
            csl = slice(tc * 512, (tc + 1) * 512)
            xk = [('xnT', t4) for t4 in range(tc * 4, tc * 4 + 4)]
            yak = [('yaT', t4) for t4 in range(tc * 4, tc * 4 + 4)]
            for oc in range(8):
                s_ = it % 2
                it += 1
                osl = slice(oc * 128, (oc + 1) * 128)
                ba, bb, bc_, bd = [4 * s_ + q for q in range(4)]
                for k in range(4):
                    P.pe(lambda e, k=k, osl=osl, csl=csl, ba=ba: e.matmul(ps[ba], lhsT=Wps[:, k, osl], rhs=ysT[:, k, csl], start=(k == 0), stop=(k == 3)),
                         reads=[('Wps', oc)] + [('ysT', k, tc)], writes=['ps%d' % ba])
                for k in range(8):
                    P.pe(lambda e, k=k, osl=osl, csl=csl, bb=bb: e.matmul(ps[bb], lhsT=Wg1[:, k, osl], rhs=xnT[:, k, csl], start=(k == 0), stop=(k == 7)),
                         reads=[('Wg1', oc)] + xk, writes=['ps%d' % bb])
                for k in range(4):
                    P.pe(lambda e, k=k, osl=osl, csl=csl, bc_=bc_: e.matmul(ps[bc_], lhsT=Wpa[:, k, osl], rhs=yaT[:, k, csl], start=(k == 0), stop=(k == 3)),
                         reads=[('Wpa', oc)] + yak, writes=['ps%d' % bc_])
                for k in range(8):
                    P.pe(lambda e, k=k, osl=osl, csl=csl, bd=bd: e.matmul(ps[bd], lhsT=Wg2[:, k, osl], rhs=xnT[:, k, csl], start=(k == 0), stop=(k == 7)),
                         reads=[('Wg2', oc)] + xk, writes=['ps%d' % bd])
                P.act(lambda e, s_=s_, bb=bb: e.activation(out=sg[2 * s_], in_=ps[bb], func=AF.Sigmoid), reads=['ps%d' % bb], writes=[('sg', 2 * s_)])
                P.act(lambda e, s_=s_, bd=bd: e.activation(out=sg[2 * s_ + 1], in_=ps[bd], func=AF.Sigmoid), reads=['ps%d' % bd], writes=[('sg', 2 * s_ + 1)])
                P.dve(lambda e, s_=s_, ba=ba: e.tensor_tensor(out=mm[2 * s_], in0=sg[2 * s_], in1=ps[ba], op=ALU.mult), reads=[('sg', 2 * s_), 'ps%d' % ba], writes=[('mm', 2 * s_)])
                P.dve(lambda e, s_=s_, bc_=bc_: e.tensor_tensor(out=mm[2 * s_ + 1], in0=sg[2 * s_ + 1], in1=ps[bc_], op=ALU.mult), reads=[('sg', 2 * s_ + 1), 'ps%d' % bc_], writes=[('mm', 2 * s_ + 1)])
                P.dve(lambda e, s_=s_, oc=oc, csl=csl: e.tensor_tensor(out=mergedT[:, oc, csl], in0=mm[2 * s_], in1=mm[2 * s_ + 1], op=ALU.add),
                      reads=[('mm', 2 * s_), ('mm', 2 * s_ + 1)], writes=[('mergedT', oc, tc)])

    X1_OFF = 139264
    x1 = Region(arena, X1_OFF, ARENA_BYTES).take([128, NT, D], F32)
    xn2T = xnT

    def phase_out():
        S = Region(arena, RA_.start, X1_OFF)
        Wout = S.take([128, 8, D], BF16)
        g2b = S.take([128, D], F32)
        xbuf = [S.take([128, D], F32) for _ in range(2)]
        xnb = [S.take([128, D], BF16) for _ in range(2)]
        junk = S.take([128, D], BF16)
        ss2 = S.take([128, NT], F32)
        rt2 = S.take([128, NT], F32)
        rs2 = S.take([128, NT], F32)
        woutv = w_out.rearrange("(kc p) n -> p kc n", p=128)
        for nh in range(2):
            P.dma('pool', lambda e, nh=nh: e.dma_start(out=Wout[:, :, nh * 512:(nh + 1) * 512], in_=woutv[:, :, nh * 512:(nh + 1) * 512]), writes=[('Wout', nh)], sem=('wout', nh))
        P.dma('sp', lambda e: e.dma_start(out=g2b, in_=norm2_g.to_broadcast([128, D])), writes=['g2b'], sem='g2b')
        def g_a(tt):
            b = tt % 2
            tsl = slice(tt * 128, (tt + 1) * 128)
            pq = 2 * (tt % 2)
            pso = psall[:, pq * 512:(pq + 2) * 512]
            P.dma('sp', lambda e, b=b, tsl=tsl: e.dma_start(out=xbuf[b], in_=x[tsl, :]), writes=[('xbuf2', b)], sem=('x2', b))
            for nh in range(2):
                for kc in range(8):
                    P.pe(lambda e, kc=kc, nh=nh, tsl=tsl, pq=pq: e.matmul(ps[pq + nh], lhsT=mergedT[:, kc, tsl], rhs=Wout[:, kc, nh * 512:(nh + 1) * 512], start=(kc == 0), stop=(kc == 7)),
                         reads=[('mergedT', kc, tt // 4), ('Wout', nh)], writes=['ps%d' % (pq + nh)])
            P.dve(lambda e, tt=tt, b=b, pso=pso: e.tensor_tensor(out=x1[:, tt, :], in0=pso, in1=xbuf[b], op=ALU.add),
                  reads=['ps%d' % pq, 'ps%d' % (pq + 1), ('xbuf2', b)], writes=[('x1', tt)])
            P.act(lambda e, tt=tt: e.activation(out=junk, in_=x1[:, tt, :], func=AF.Square, accum_out=ss2[:, tt:tt + 1]),
                  reads=[('x1', tt)], writes=['junk2', ('ss2', tt)])
            P.act(lambda e, tt=tt: e.activation(out=rt2[:, tt:tt + 1], in_=ss2[:, tt:tt + 1], func=AF.Sqrt, scale=1.0 / D, bias=EPS),
                  reads=[('ss2', tt)], writes=[('rt2', tt)])
            P.dve(lambda e, tt=tt: e.reciprocal(out=rs2[:, tt:tt + 1], in_=rt2[:, tt:tt + 1]), reads=[('rt2', tt)], writes=[('rs2', tt)])
            P.dve(lambda e, b=b, tt=tt: e.scalar_tensor_tensor(out=xnb[b], in0=x1[:, tt, :], scalar=rs2[:, tt:tt + 1], in1=g2b, op0=ALU.mult, op1=ALU.mult),
                  reads=[('x1', tt), ('rs2', tt), 'g2b'], writes=[('xnb2', b)])

        def g_b(tt):
            b = tt % 2
            tsl = slice(tt * 128, (tt + 1) * 128)
            pt = 6 + b
            for kc in range(8):
                P.pe(lambda e, b=b, kc=kc, pt=pt: e.transpose(out=psb(pt)[:, kc, :], in_=xnb[b][:, kc * 128:(kc + 1) * 128], identity=ident_bf),
                     reads=[('xnb2', b), 'ident_bf'], writes=['ps%d' % pt])
            P.act(lambda e, tsl=tsl, pt=pt: e.activation(out=xn2T[:, :, tsl], in_=psb(pt), func=AF.Copy), reads=['ps%d' % pt], writes=[('xn2T', tt)])

        g_a(0)
        for tt in range(NT):
            if tt + 1 < NT:
                g_a(tt + 1)
            g_b(tt)

    def phase_ffn():
        S = Region(arena, RCm.start, X1_OFF)
        hT = S.take([128, NFC, 1024], BF16)
        Wd = [S.take([128, NFC, 512], BF16) for _ in range(2)]
        Wg = [S.take([128, 8, 128], BF16) for _ in range(2)]
        Wu = [S.take([128, 8, 128], BF16) for _ in range(2)]
        sgl = [S.take([128, 512], F32) for _ in range(2)]
        obuf = sgl
        wgv = w_ffn_gate.rearrange("(kc p) n -> p kc n", p=128)
        wuv = w_ffn_up.rearrange("(kc p) n -> p kc n", p=128)
        wdv = w_ffn_down.rearrange("(fc p) n -> p fc n", p=128)
        it = 0
        oi = 0
        for half in range(2):
            for fc in range(NFC):
                wb = fc % 2
                fsl = slice(fc * 128, (fc + 1) * 128)
                P.dma('pool', lambda e, wb=wb, fsl=fsl: e.dma_start(out=Wg[wb], in_=wgv[:, :, fsl]), writes=[('Wg', wb)], sem=('wg', wb))
                P.dma('pool', lambda e, wb=wb, fsl=fsl: e.dma_start(out=Wu[wb], in_=wuv[:, :, fsl]), writes=[('Wu', wb)], sem=('wu', wb))
                if fc in (8, 16):
                    nh_ = 0 if fc == 8 else 1
                    P.dma('pool', lambda e, nh_=nh_: e.dma_start(out=Wd[nh_], in_=wdv[:, :, nh_ * 512:(nh_ + 1) * 512]), writes=[('Wd', nh_)], sem=('wd', nh_))
                for tc2 in range(2):
                    c0 = half * 1024 + tc2 * 512
                    csl = slice(c0, c0 + 512)
                    xk = [('xn2T', t4) for t4 in range(c0 // 128, c0 // 128 + 4)]
                    s_ = it % 2
                    it += 1
                    bg, bu = 2 * s_, 2 * s_ + 1
                    for kc in range(8):
                        P.pe(lambda e, kc=kc, wb=wb, csl=csl, bg=bg: e.matmul(ps[bg], lhsT=Wg[wb][:, kc, :], rhs=xn2T[:, kc, csl], start=(kc == 0), stop=(kc == 7)),
                             reads=[('Wg', wb)] + xk, writes=['ps%d' % bg])
                    for kc in range(8):
                        P.pe(lambda e, kc=kc, wb=wb, csl=csl, bu=bu: e.matmul(ps[bu], lhsT=Wu[wb][:, kc, :], rhs=xn2T[:, kc, csl], start=(kc == 0), stop=(kc == 7)),
                             reads=[('Wu', wb)] + xk, writes=['ps%d' % bu])
                    P.act(lambda e, s_=s_, bg=bg: e.activation(out=sgl[s_], in_=ps[bg], func=AF.Silu), reads=['ps%d' % bg], writes=[('sgl', s_)])
                    P.dve(lambda e, s_=s_, bu=bu, fc=fc, tc2=tc2: e.tensor_tensor(out=hT[:, fc, tc2 * 512:(tc2 + 1) * 512], in0=sgl[s_], in1=ps[bu], op=ALU.mult),
                          reads=[('sgl', s_), 'ps%d' % bu], writes=[('hT', fc, tc2)])
            for nh in range(2):
                nsl = slice(nh * 512, (nh + 1) * 512)
                for t8 in range(8):
                    tt = half * 8 + t8
                    tsl = slice(tt * 128, (tt + 1) * 128)
                    pb = 4 + (oi % 2)
                    ob = oi % 2
                    oi += 1
                    for fc in range(NFC):
                        P.pe(lambda e, fc=fc, t8=t8, pb=pb, nh=nh: e.matmul(ps[pb], lhsT=hT[:, fc, t8 * 128:(t8 + 1) * 128], rhs=Wd[nh][:, fc, :], start=(fc == 0), stop=(fc == NFC - 1)),
                             reads=[('hT', fc, t8 // 4), ('Wd', nh)], writes=['ps%d' % pb])
                    P.dve(lambda e, ob=ob, pb=pb, tt=tt, nsl=nsl: e.tensor_tensor(out=obuf[ob], in0=ps[pb], in1=x1[:, tt, nsl], op=ALU.add),
                          reads=['ps%d' % pb, ('x1', tt)], writes=[('sgl', ob)])
                    P.dma('sp', lambda e, ob=ob, tsl=tsl, nsl=nsl: e.dma_start(out=y[tsl, nsl], in_=obuf[ob]), reads=[('sgl', ob)], writes=[('y', ob)], sem=('yo', ob))

    if 'stop_s5' not in dbg:
        P.barrier()
        phase_merge()
        dump("mergedT", mergedT, [128, 8, L])
        P.barrier()
        phase_out()
        dump("x1", x1, [128, NT, D])
        P.barrier()
        phase_ffn()

    outs = [('y', 0), ('y', 1)] + ['dbg_' + n for n in dbg_out]
    P.final_wait(outs)
    P.emit()
    return nc, dbg_out


_NC_CACHE = {}


def kernel(**inputs):
    if 'nc' not in _NC_CACHE:
        _NC_CACHE['nc'] = build_nc()[0]
    nc = _NC_CACHE['nc']
    x = np.ascontiguousarray(np.asarray(inputs['x'], dtype=np.float32))
    shared = {}
    for k, v in inputs.items():
        if k == 'x':
            continue
        a = np.asarray(v, dtype=np.float32)
        a = a.reshape(a.shape[1:])
        if k in ('norm1_g', 'norm2_g', 'q_norm_g', 'k_norm_g', 'idx_k_norm_g'):
            a = a.reshape(1, -1)
        shared[k] = np.ascontiguousarray(a)
    in_maps = []
    for c in range(8):
        m = dict(shared)
        m['x'] = x[c]
        in_maps.append(m)
    res = run_bass_kernel_spmd(nc, in_maps, core_ids=list(range(8)))
    return np.stack([r['y'] for r in res.results], axis=0)
```
